# Optimizing a Trainium2 kernel written in Bass

```python
import jax, jax.numpy as jnp
from jax import lax
import numpy as np

D_MODEL = 2048
BATCH = 4
SEQ = 2048
DEPTH = 1

CHUNK = 64
H_A = 8
DK_A = 128
DV_A = 128
D_A = H_A * DK_A
H_B = 8
DH_B = 128
D_B = H_B * DH_B
N_PAST_CHUNKS = 8
BAND = N_PAST_CHUNKS + 1
REL_FUTURE = CHUNK - 1
REL_PAST = 2 * CHUNK - 1
N_REL = REL_FUTURE + REL_PAST + 1
D_FF = -(-8 * D_MODEL // (3 * 256)) * 256
N_IN = 4 * D_A + 3 * D_B + 2 * D_MODEL
EPS = 1e-6

kernel_name = "hybrid_hgrn2_chunkattn_gated_block"


def rms_norm(x, gain):
    xf = x.astype(jnp.float32)
    y = xf * lax.rsqrt(jnp.mean(xf * xf, axis=-1, keepdims=True) + EPS)
    return (y * gain.astype(jnp.float32)).astype(x.dtype)


def hgrn_lower_bounds(lb_logits):
    p = jax.nn.softmax(lb_logits.astype(jnp.float32), axis=0)
    return jnp.cumsum(p, axis=0)[:DEPTH]


def hgrn2_mixer(q, f_logit, i, g, lb, out_gain):
    B, T, _ = q.shape
    n_chunks = T // CHUNK
    f32 = jnp.float32
    lbf = lb.astype(f32)
    f = lbf + (1.0 - lbf) * jax.nn.sigmoid(f_logit.astype(f32))
    log_f = jnp.log(f)
    k = 1.0 - f
    qs = jax.nn.silu(q.astype(f32))

    def to_chunks(t, dh):
        return t.reshape(B, n_chunks, CHUNK, H_A, dh).transpose(1, 0, 3, 2, 4)

    qc, kc, lfc = to_chunks(qs, DK_A), to_chunks(k, DK_A), to_chunks(log_f, DK_A)
    vc = to_chunks(i.astype(f32), DV_A)
    causal = jnp.tril(jnp.ones((CHUNK, CHUNK), dtype=bool))[:, :, None]

    def step(S, inp):
        qj, kj, lfj, vj = inp
        b = jnp.cumsum(lfj, axis=2)
        o_inter = jnp.einsum('bhtk,bhkv->bhtv', qj * jnp.exp(b), S)
        rel = jnp.where(causal, b[:, :, :, None, :] - b[:, :, None, :, :], -jnp.inf)
        decay = jnp.exp(rel)
        scores = jnp.einsum('bhtk,bhsk,bhtsk->bhts', qj, kj, decay)
        o_intra = jnp.einsum('bhts,bhsv->bhtv', scores, vj)
        b_last = b[:, :, -1:, :]
        S_new = jnp.exp(b_last[:, :, 0, :, None]) * S + jnp.einsum(
            'bhsk,bhsv->bhkv', kj * jnp.exp(b_last - b), vj)
        return S_new, o_inter + o_intra

    S0 = jnp.zeros((B, H_A, DK_A, DV_A), f32)
    _, o = lax.scan(step, S0, (qc, kc, lfc, vc))
    o = o.transpose(1, 0, 3, 2, 4).reshape(B, T, H_A, DV_A)
    o = o * lax.rsqrt(jnp.mean(o * o, axis=-1, keepdims=True) + EPS)
    o = o.reshape(B, T, D_A) * out_gain.astype(f32)
    o = o * jax.nn.silu(g.astype(f32))
    return o.astype(q.dtype)


def head_rms_norm(t, gain):
    tf = t.astype(jnp.float32)
    y = tf * lax.rsqrt(jnp.mean(tf * tf, axis=-1, keepdims=True) + EPS)
    return y * gain.astype(jnp.float32)


def chunked_relpos_attention(q, k, v, q_gain, k_gain, rel_bias):
    B, T, _ = q.shape
    n_chunks = T // CHUNK

    def heads(t):
        return t.reshape(B, n_chunks, CHUNK, H_B, DH_B).transpose(0, 3, 1, 2, 4)

    qh = head_rms_norm(heads(q), q_gain)
    kh = head_rms_norm(heads(k), k_gain)
    vh = heads(v).astype(jnp.float32)

    pad = ((0, 0), (0, 0), (N_PAST_CHUNKS, 0), (0, 0), (0, 0))
    band_idx = jnp.arange(n_chunks)[:, None] + jnp.arange(BAND)[None, :]
    k_band = jnp.pad(kh, pad)[:, :, band_idx].reshape(B, H_B, n_chunks, BAND * CHUNK, DH_B)
    v_band = jnp.pad(vh, pad)[:, :, band_idx].reshape(B, H_B, n_chunks, BAND * CHUNK, DH_B)

    q_pos = jnp.arange(n_chunks)[:, None] * CHUNK + jnp.arange(CHUNK)[None, :]
    k_chunk = band_idx - N_PAST_CHUNKS
    k_pos = (k_chunk[:, :, None] * CHUNK + jnp.arange(CHUNK)[None, None, :]).reshape(
        n_chunks, BAND * CHUNK)
    valid = k_pos >= 0
    dist = q_pos[:, :, None] - k_pos[:, None, :]
    rel_idx = jnp.clip(dist, -REL_FUTURE, REL_PAST) + REL_FUTURE
    bias = rel_bias.astype(jnp.float32)[:, rel_idx]

    scale = DH_B ** -0.5
    scores = jnp.einsum('bhnqd,bhnkd->bhnqk', qh, k_band) * scale + bias[None]
    scores = jnp.where(valid[None, None, :, None, :], scores, -jnp.inf)
    p = jax.nn.softmax(scores, axis=-1)
    o = jnp.einsum('bhnqk,bhnkd->bhnqd', p, v_band)
    return o.transpose(0, 2, 3, 1, 4).reshape(B, T, D_B).astype(q.dtype)


def setup_inputs(seed: int = 0) -> dict:
    key = jax.random.key(seed)
    ks = jax.random.split(key, 16)
    f32 = jnp.float32

    def nrm(k, shape, scale):
        return jax.random.normal(k, shape, f32) * scale

    return {
        "x": nrm(ks[0], (BATCH, SEQ, D_MODEL), 1.0),
        "w_in": nrm(ks[1], (DEPTH, D_MODEL, N_IN), D_MODEL ** -0.5),
        "b_gate": nrm(ks[2], (DEPTH, 2 * D_MODEL), 0.02),
        "norm_mix": 1.0 + nrm(ks[3], (DEPTH, D_MODEL), 0.02),
        "norm_ffn": 1.0 + nrm(ks[4], (DEPTH, D_MODEL), 0.02),
        "hgrn_lb_logits": nrm(ks[5], (DEPTH + 1, D_A), 0.5),
        "hgrn_out_gain": 1.0 + nrm(ks[6], (DEPTH, D_A), 0.02),
        "q_gain": 1.0 + nrm(ks[7], (DEPTH, DH_B), 0.02),
        "k_gain": 1.0 + nrm(ks[8], (DEPTH, DH_B), 0.02),
        "rel_bias": nrm(ks[9], (DEPTH, H_B, N_REL), 0.1),
        "w_proj_a": nrm(ks[10], (DEPTH, D_A, D_MODEL), D_A ** -0.5),
        "w_proj_b": nrm(ks[11], (DEPTH, D_B, D_MODEL), D_B ** -0.5),
        "w_out": nrm(ks[12], (DEPTH, D_MODEL, D_MODEL), D_MODEL ** -0.5),
        "w_ffn_in": nrm(ks[13], (DEPTH, D_MODEL, 2 * D_FF), D_MODEL ** -0.5),
        "w_ffn_out": nrm(ks[14], (DEPTH, D_FF, D_MODEL), D_FF ** -0.5),
    }


def reference(x, w_in, b_gate, norm_mix, norm_ffn, hgrn_lb_logits, hgrn_out_gain,
              q_gain, k_gain, rel_bias, w_proj_a, w_proj_b, w_out, w_ffn_in, w_ffn_out):
    lower_bounds = hgrn_lower_bounds(hgrn_lb_logits)
    split_pts = [D_A, 2 * D_A, 3 * D_A, 4 * D_A,
                 4 * D_A + D_B, 4 * D_A + 2 * D_B, 4 * D_A + 3 * D_B]
    for l in range(DEPTH):
        h = rms_norm(x, norm_mix[l])
        proj = jnp.einsum('btd,dn->btn', h, w_in[l])
        q_a, f_a, i_a, g_a, q_b, k_b, v_b, gate_logits = jnp.split(proj, split_pts, axis=-1)
        gates = jax.nn.sigmoid((gate_logits + b_gate[l]).astype(jnp.float32)).astype(x.dtype)
        gate_a, gate_b = jnp.split(gates, 2, axis=-1)

        y_a = hgrn2_mixer(q_a, f_a, i_a, g_a, lower_bounds[l], hgrn_out_gain[l])
        y_b = chunked_relpos_attention(q_b, k_b, v_b, q_gain[l], k_gain[l], rel_bias[l])

        merged = (gate_a * jnp.einsum('btc,cd->btd', y_a, w_proj_a[l])
                  + gate_b * jnp.einsum('btc,cd->btd', y_b, w_proj_b[l]))
        x = x + jnp.einsum('btd,de->bte', merged, w_out[l])

        h = rms_norm(x, norm_ffn[l])
        gate_up = jnp.einsum('btd,df->btf', h, w_ffn_in[l])
        ff_gate, ff_up = jnp.split(gate_up, 2, axis=-1)
        x = x + jnp.einsum('btf,fd->btd', jax.nn.silu(ff_gate) * ff_up, w_ffn_out[l])
    return x
```

```python
import contextlib
import os
import numpy as np
import concourse.bass as bass
import concourse.mybir as mybir
from concourse.bass_utils import run_bass_kernel_spmd

F32 = mybir.dt.float32
BF16 = mybir.dt.bfloat16
AF = mybir.ActivationFunctionType
ALU = mybir.AluOpType
AX = mybir.AxisListType

ENGS = ("pe", "act", "dve", "pool", "sp")

D = 2048
T = 1024
NPOS = 2048
DFF = 5632
NFC = 44
FG = 11
NEG = -30000.0


class Buf:
    __slots__ = ("name", "writers", "readers", "prev")

    def __init__(self, name=""):
        self.name = name
        self.writers = []
        self.readers = []
        self.prev = []


class Op:
    __slots__ = ("eng", "fn", "deps", "idx", "is_dma", "slot", "val", "signaled")

    def __init__(self, eng, fn, is_dma=False, slot=None):
        self.eng = eng
        self.fn = fn
        self.deps = []
        self.idx = -1
        self.is_dma = is_dma
        self.slot = slot
        self.val = 0
        self.signaled = False


class Sched:
    def __init__(self, nc):
        self.nc = nc
        self.q = {e: [] for e in ENGS}
        self.slot_count = {}
        self.slot_last = {}
        self.bar = {e: [] for e in ENGS}

    def _hazards(self, op, reads, writes, pwrites):
        deps = op.deps
        for b in reads:
            deps.extend(b.writers)
            b.readers.append(op)
        for b in writes:
            deps.extend(b.readers)
            deps.extend(b.writers)
            deps.extend(b.prev)
            b.prev = []
            b.writers = [op]
            b.readers = []
        for b in pwrites:
            deps.extend(b.prev)
            deps.extend(b.readers)
            b.writers.append(op)

    def _add(self, o, eng, reads, writes, pwrites, deps):
        o.deps.extend(deps)
        if self.bar[eng]:
            o.deps.extend(self.bar[eng])
            self.bar[eng] = []
        self._hazards(o, reads, writes, pwrites)
        o.idx = len(self.q[eng])
        self.q[eng].append(o)
        return o

    def op(self, eng, fn, reads=(), writes=(), pwrites=(), deps=()):
        return self._add(Op(eng, fn), eng, reads, writes, pwrites, deps)

    def dma(self, eng, out, in_, slot, reads=(), writes=(), pwrites=(), deps=()):
        def fn(e):
            return e.dma_start(out=out, in_=in_)
        o = Op(eng, fn, is_dma=True, slot=slot)
        c = self.slot_count.get(slot, 0) + 1
        self.slot_count[slot] = c
        o.val = 16 * c
        self.slot_last[slot] = o
        return self._add(o, eng, reads, writes, pwrites, deps)

    @staticmethod
    def rotate(b):
        b.prev = b.prev + b.readers + b.writers
        b.readers = []
        b.writers = []

    def barrier(self):
        lasts = [self.q[e][-1] for e in ENGS if self.q[e] and not self.q[e][-1].is_dma]
        lasts = []
        for e in ENGS:
            for o in reversed(self.q[e]):
                if not o.is_dma:
                    lasts.append(o)
                    break
        lasts.extend(self.slot_last.values())
        for e in ENGS:
            if e != "pool":
                self.bar[e] = list(lasts)

    def emit(self, final_waits=()):
        nc = self.nc
        for e in ENGS:
            for o in self.q[e]:
                best = {}
                for d in o.deps:
                    if d is o:
                        continue
                    if d.is_dma:
                        key = ("slot", d.slot)
                        v = d.val
                    else:
                        if d.eng == e:
                            if e == "pe":
                                continue
                            if o.idx - d.idx > 2:
                                continue
                        key = ("eng", d.eng)
                        v = d.idx
                    if key not in best or v > best[key][0]:
                        best[key] = (v, d)
                o.deps = [d for (_, d) in best.values()]
                for d in o.deps:
                    d.signaled = True
        for o in final_waits:
            o.signaled = True
        for e in ENGS:
            c = 0
            for o in self.q[e]:
                if o.is_dma:
                    continue
                if o.signaled:
                    c += 1
                    o.val = c
        slots = sorted(self.slot_count.keys())
        with contextlib.ExitStack() as st:
            esem = {e: st.enter_context(nc.semaphore("s_" + e)) for e in ENGS}
            ssem = {s: st.enter_context(nc.semaphore("d_" + str(s))) for s in slots}
            block = st.enter_context(nc.Block())
            deco = {"pe": block.tensor, "act": block.scalar, "dve": block.vector,
                    "pool": block.gpsimd, "sp": block.sync}

            def make(e):
                def body(eng):
                    waited = {}
                    for o in self.q[e]:
                        for d in o.deps:
                            if d.is_dma:
                                sem = ssem[d.slot]
                                key = ("slot", d.slot)
                            else:
                                sem = esem[d.eng]
                                key = ("eng", d.eng)
                            if waited.get(key, 0) >= d.val:
                                continue
                            eng.wait_ge(sem, d.val)
                            waited[key] = d.val
                        ins = o.fn(eng)
                        if o.is_dma:
                            ins.then_inc(ssem[o.slot], 16)
                        elif o.signaled:
                            ins.then_inc(esem[e], 1)
                    if e == "sp":
                        for o in final_waits:
                            if o.is_dma:
                                eng.wait_ge(ssem[o.slot], o.val)
                            else:
                                eng.wait_ge(esem[o.eng], o.val)
                return body

            for e in ENGS:
                deco[e](make(e))


NPP = 64
PP_L0, PP_L1, PP_OG, PP_BGA, PP_BGB, PP_QG, PP_KG, PP_PM, PP_EPS = 0, 8, 16, 24, 40, 56, 57, 58, 59


def build(stop=None):
    nc = bass.Bass("TRN2", target_bir_lowering=False)
    dt = nc.dram_tensor
    xo = dt("xo", [T, D], F32, kind="ExternalInput").ap()
    xp = dt("xp", [T, D], F32, kind="ExternalInput").ap()
    ppd = dt("pp", [128, NPP], F32, kind="ExternalInput").ap()
    cst = dt("cst", [128, 128 + 128 + 512 + 512 + 640], F32, kind="ExternalInput").ap()
    gbc = dt("gbc", [2, 128, D], F32, kind="ExternalInput").ap()
    biasd = dt("biasT", [128, 8 * 640], F32, kind="ExternalInput").ap()
    wA = dt("wA", [16, 128, 16 * 256], F32, kind="ExternalInput").ap()
    wBkv = dt("wBkv", [8, 128, 16 * 256], F32, kind="ExternalInput").ap()
    wBq = dt("wBq", [8, 128, 16 * 128], F32, kind="ExternalInput").ap()
    wCg = dt("wCg", [16, 128, 16 * 256], F32, kind="ExternalInput").ap()
    wCp = dt("wCp", [16, 128, 8 * 256], F32, kind="ExternalInput").ap()
    wD = dt("wD", [8, 128, 16 * 256], F32, kind="ExternalInput").ap()
    wF1 = dt("wF1", [NFC, 128, 16 * 256], F32, kind="ExternalInput").ap()
    wF2 = dt("wF2", [NFC // FG, 8, 128, FG * 256], F32, kind="ExternalInput").ap()
    yout = dt("y", [T, D], F32, kind="ExternalOutput").ap()
    dbg_out = {}

    def finish(dumps):
        last = []
        for i, (name, ap, shape, dty, bufs) in enumerate(dumps):
            dd = dt("dbg_" + name, shape, dty, kind="ExternalOutput").ap()
            last.append(S.dma("sp", dd, ap, slot=f"dbg{i}", reads=bufs))
        S.emit(final_waits=last)
        return nc, [d[0] for d in dumps]

    S = Sched(nc)
    A_ = nc.alloc_sbuf_tensor

    R1 = A_("R1", [128, 16384], F32)
    R2 = A_("R2", [128, 16384], BF16)
    R3 = A_("R3", [128, 16384], BF16)
    R4 = A_("R4", [128, 8192], F32)
    WS = [A_(f"WS{i}", [128, 4096], BF16) for i in range(2)]
    WB = [Buf(f"WS{i}") for i in range(2)]
    pp = A_("pp_sb", [128, NPP], F32)
    ppx = A_("ppx", [128, 32], F32)
    cstf = A_("cstf", [128, 128 + 128 + 512 + 512 + 640], F32)
    ident = A_("ident", [128, 128], BF16)
    ones = A_("ones", [128, 128], BF16)
    cmask = A_("cmask", [64, 512], F32)
    Bpp, Bppx, Bcst, Bconst = Buf(), Buf(), Buf(), Buf()
    scanm = cstf[:, 768:1280]
    smask = cstf[:, 1280:1920]

    PD = [nc.alloc_psum_tensor(f"PD{i}", [128, 1024], F32) for i in range(4)]
    PB = [Buf(f"bank{i}") for i in range(8)]

    def bank(b):
        return PD[b // 2][:, (b % 2) * 512:(b % 2) * 512 + 512]

    def bankbf(b):
        return bank(b).bitcast(BF16)

    state = {"bank": 0, "w": 0}

    def nb():
        b = state["bank"]
        state["bank"] = (b + 1) % 8
        S.rotate(PB[b])
        return b

    def nb2():
        if state["bank"] % 2:
            state["bank"] = (state["bank"] + 1) % 8
        b = state["bank"]
        state["bank"] = (b + 2) % 8
        S.rotate(PB[b])
        S.rotate(PB[b + 1])
        return b

    wlist = []
    for h in range(8):
        wlist.append((wA[2 * h], 4096))
        wlist.append((wA[2 * h + 1], 4096))
    for h in range(8):
        wlist.append((wBkv[h], 4096))
        wlist.append((wBq[h], 2048))
    for c in range(16):
        wlist.append((wCg[c], 4096))
        wlist.append((wCp[c], 2048))
    for g in range(8):
        wlist.append((wD[g], 4096))
    for g in range(NFC // FG):
        for j in range(FG):
            wlist.append((wF1[g * FG + j], 4096))
        for cg in range(8):
            wlist.append((wF2[g, cg], FG * 256))
    wstate = {"issued": 0, "taken": 0}

    def w_issue():
        i = wstate["issued"]
        if i >= len(wlist):
            return
        src, n = wlist[i]
        s = i % 2
        S.dma("pool", WS[s][:, 0:n], src, slot=f"w{s}", writes=[WB[s]])
        wstate["issued"] = i + 1

    def w_next():
        i = wstate["taken"]
        while wstate["issued"] <= i:
            w_issue()
        wstate["taken"] = i + 1
        return WS[i % 2], WB[i % 2]

    def w_prefetch():
        while wstate["issued"] < min(len(wlist), wstate["taken"] + 2):
            w_issue()

    def ACT(out, in_, func, R=(), W=(), PW=(), scale=1.0, bias=None, accum_out=None):
        kw = {}
        if bias is not None:
            kw["bias"] = bias
        if accum_out is not None:
            kw["accum_out"] = accum_out
        return S.op("act", lambda e: e.activation(out=out, in_=in_, func=func, scale=scale, **kw),
                    reads=R, writes=W, pwrites=PW)

    def TS(out, in0, s1, s2, op0, op1, R=(), W=(), PW=(), eng="dve"):
        if s2 is None:
            return S.op(eng, lambda e: e.tensor_single_scalar(out=out, in_=in0, scalar=s1, op=op0),
                        reads=R, writes=W, pwrites=PW)
        return S.op(eng, lambda e: e.tensor_scalar(out=out, in0=in0, scalar1=s1, scalar2=s2, op0=op0, op1=op1),
                    reads=R, writes=W, pwrites=PW)

    def TT(out, in0, in1, op, R=(), W=(), PW=(), eng="dve"):
        return S.op(eng, lambda e: e.tensor_tensor(out=out, in0=in0, in1=in1, op=op),
                    reads=R, writes=W, pwrites=PW)

    def STT(out, in0, scalar, in1, op0, op1, R=(), W=(), PW=()):
        return S.op("dve", lambda e: e.scalar_tensor_tensor(out=out, in0=in0, scalar=scalar, in1=in1,
                                                            op0=op0, op1=op1),
                    reads=R, writes=W, pwrites=PW)

    def CP(eng, out, in_, R=(), W=(), PW=()):
        if eng == "act":
            return ACT(out, in_, AF.Copy, R=R, W=W, PW=PW)
        return S.op(eng, lambda e: e.tensor_copy(out=out, in_=in_), reads=R, writes=W, pwrites=PW)

    def MM(out, lhsT, rhs, start, stop, R=(), PW=()):
        return S.op("pe", lambda e: e.matmul(out, lhsT, rhs, start=start, stop=stop), reads=R, pwrites=PW)

    def TR(out, in_, R=(), PW=()):
        return S.op("pe", lambda e: e.transpose(out, in_, ident[:]), reads=list(R) + [Bconst], pwrites=PW)

    flip = {"i": 0}

    def alt():
        flip["i"] ^= 1
        return "act" if flip["i"] else "dve"

    S.dma("sp", pp[:], ppd, slot="c0", writes=[Bpp])
    S.dma("sp", cstf[:], cst, slot="c1", writes=[Bcst])
    CP("dve", ident[:], cstf[:, 0:128], R=[Bcst], W=[Bconst])
    CP("dve", ones[:], cstf[:, 128:256], R=[Bcst], PW=[Bconst])
    CP("dve", cmask[:], cstf[0:64, 256:768], R=[Bcst], PW=[Bconst])
    TT(ppx[:, 0:8], pp[:, PP_L0:PP_L0 + 8], pp[:, PP_L1:PP_L1 + 8], ALU.subtract, R=[Bpp], W=[Bppx])
    ACT(ppx[:, 0:8], ppx[:, 0:8], AF.Sigmoid, R=[Bppx], W=[Bppx])
    TS(ppx[:, 8:16], ppx[:, 0:8], -1.0, 1.0, ALU.mult, ALU.add, R=[Bppx], W=[Bppx])
    TS(ppx[:, 16:24], ppx[:, 0:8], 1.0, -1.0, ALU.mult, ALU.add, R=[Bppx], W=[Bppx])
    epsc = pp[:, PP_EPS:PP_EPS + 1]

    hTp = R1[:, 0:8192].bitcast(BF16)
    hTo = R2
    BhTp, BhTo = Buf("hTp"), Buf("hTo")
    yaT = R3[:, 0:8192]
    ybT = R3[:, 8192:16384]
    ByaT, BybT = Buf("yaT"), Buf("ybT")

    def hsrc(kc, t0, n):
        if t0 < 1024:
            return hTp[:, kc * 1024 + t0: kc * 1024 + t0 + n], BhTp
        t0 -= 1024
        return hTo[:, kc * 1024 + t0: kc * 1024 + t0 + n], BhTo

    def norm_phase(tiles, gsel, tmp_base):
        XT = [R4[:, 0:2048], R4[:, 2048:4096]]
        BXT = [Buf("xt0"), Buf("xt1")]
        xn = R4[:, 4096:5120].bitcast(BF16)
        junk = R4[:, 5120:6144].bitcast(BF16)
        gain = R4[:, 6144:8192]
        stat = ppx[:, 24:28]
        Bxn, Bjunk, Bgain, Bstat = Buf(), Buf(), Buf(), Buf()
        S.dma("sp", gain, gbc[gsel], slot="c2", writes=[Bgain])
        for i, (kind, r) in enumerate(tiles):
            if kind == "X":
                xt = R1[:, r * 2048:(r + 1) * 2048]
                bx = BX[r]
            else:
                xt = XT[i % 2]
                bx = BXT[i % 2]
                src = xp if kind == "p" else xo
                S.dma("sp", xt, src[r * 128:(r + 1) * 128, :], slot=f"x{i % 2}", writes=[bx])
            ACT(junk, xt, AF.Square, R=[bx], W=[Bjunk, Bstat], accum_out=stat[:, 0:1])
            ACT(stat[:, 1:2], stat[:, 0:1], AF.Ln, R=[Bstat], W=[Bstat], scale=1.0 / D, bias=epsc)
            ACT(stat[:, 2:3], stat[:, 1:2], AF.Exp, R=[Bstat], W=[Bstat], scale=-0.5)
            STT(xn, xt, stat[:, 2:3], gain, ALU.mult, ALU.mult, R=[bx, Bstat, Bgain], W=[Bxn])
            for half in range(2):
                b = nb()
                for j in range(8):
                    kc = half * 8 + j
                    TR(bankbf(b)[:, j * 128:(j + 1) * 128], xn[:, kc * 128:(kc + 1) * 128], R=[Bxn], PW=[PB[b]])
                if kind == "p":
                    dst, bd = hTp, BhTp
                else:
                    dst, bd = hTo, BhTo
                d3 = dst.rearrange("p (k t) -> p k t", t=1024)[:, half * 8:half * 8 + 8, r * 128:(r + 1) * 128]
                s3 = bankbf(b).rearrange("p (k t) -> p k t", t=128)
                CP(alt(), d3, s3, R=[PB[b]], PW=[bd])

    def proj_fm(wt, wb, cb, gc, kcn, src_fn, t0, n, b):
        for kc in range(kcn):
            rhs, rb = src_fn(kc, t0, n)
            MM(bank(b)[:, 0:n], wt[:, kc * gc + cb * 128: kc * gc + cb * 128 + 128], rhs,
               start=(kc == 0), stop=(kc == kcn - 1), R=[wb, rb], PW=[PB[b]])

    BX = [Buf(f"X{i}") for i in range(8)]
    w_issue()
    w_issue()
    norm_phase([("p", r) for r in range(8)] + [("o", r) for r in range(8)], 0, None)
    if stop == "0":
        return finish([("hTo", hTo[:], [128, 16384], BF16, [BhTo]), ("hTp", hTp, [128, 16384], BF16, [BhTp])])
    S.barrier()

    def r1f(off, n):
        return R1[:, 8192 + off: 8192 + off + n]

    def r4f(off, n):
        return R4[:, off: off + n]

    v_tm = r1f(0, 2048).bitcast(BF16)
    kh_tm = r1f(2048, 2048).bitcast(BF16)
    ktT = r1f(4096, 512).bitcast(BF16)
    qtT = r1f(4608, 512).bitcast(BF16)
    gsil = r1f(5120, 1024)
    Sbf = r1f(6144, 1024).bitcast(BF16)
    smT = r1f(7168, 512).bitcast(BF16)
    Sst = r1f(7680, 128)
    dl = r1f(7808, 64)
    sig = r4f(0, 512)
    omf = r4f(512, 512)
    lf = r4f(1024, 512)
    bcs = r4f(1536, 512)
    Eb = r4f(2048, 512)
    ek = r4f(2560, 512)
    ek2 = r4f(3072, 512)
    khT = r4f(3584, 256).bitcast(BF16)
    viT = r4f(3840, 256).bitcast(BF16)
    qs = r4f(4096, 512)
    eqb = r4f(4608, 512)
    osq = r4f(5120, 256).bitcast(BF16)
    rstd = r4f(5376, 512)
    t1 = r4f(5888, 512)
    bown = r4f(6400, 1024)
    rown = r4f(7424, 16)
    Bt = {n: Buf(n) for n in ["v_tm", "kh_tm", "ktT", "qtT", "gsil", "Sbf", "smT", "Sst", "dl", "sig", "omf", "lf",
                              "bcs", "Eb", "ek", "ek2", "khT", "viT", "qs", "eqb", "osq", "rstd", "t1", "bown"]}

    for h in range(8):
        lbc = ppx[:, h:h + 1]
        omlc = ppx[:, 8 + h:9 + h]
        nomlc = ppx[:, 16 + h:17 + h]
        w0, wb0 = w_next()
        S.rotate(Bt["v_tm"]); S.rotate(Bt["kh_tm"]); S.rotate(Bt["ktT"]); S.rotate(Bt["dl"]); S.rotate(Bt["bown"])
        for tb in range(4):
            t0 = tb * 512
            bf_ = nb()
            proj_fm(w0, wb0, 0, 256, 16, hsrc, t0, 512, bf_)
            ACT(sig, bank(bf_), AF.Sigmoid, R=[PB[bf_]], W=[Bt["sig"]])
            bi_ = nb()
            proj_fm(w0, wb0, 1, 256, 16, hsrc, t0, 512, bi_)
            CP("dve", viT, bank(bi_), R=[PB[bi_]], W=[Bt["viT"]])
            TS(omf, sig, nomlc, omlc, ALU.mult, ALU.add, R=[Bt["sig"], Bppx], W=[Bt["omf"]])
            TS(sig, sig, omlc, lbc, ALU.mult, ALU.add, R=[Bt["sig"], Bppx], W=[Bt["sig"]])
            ACT(lf, sig, AF.Ln, R=[Bt["sig"]], W=[Bt["lf"]])
            S.op("dve", lambda e: e.tensor_tensor_scan(out=bcs, data0=scanm, data1=lf, initial=0.0,
                                                       op0=ALU.mult, op1=ALU.add),
                 reads=[Bt["lf"], Bcst], writes=[Bt["bcs"]])
            b3 = bcs.rearrange("p (c t) -> p c t", t=64)
            TT(Eb.rearrange("p (c t) -> p c t", t=64), b3, b3[:, :, 31:32].to_broadcast([128, 8, 64]),
               ALU.subtract, R=[Bt["bcs"]], W=[Bt["Eb"]])
            ACT(ek, Eb, AF.Exp, R=[Bt["Eb"]], W=[Bt["ek"]], scale=-1.0)
            TT(Eb.rearrange("p (c t) -> p c t", t=64), b3, b3[:, :, 63:64].to_broadcast([128, 8, 64]),
               ALU.subtract, R=[Bt["bcs"], Bt["ek"]], W=[Bt["Eb"]])
            ACT(ek2, Eb, AF.Exp, R=[Bt["Eb"]], W=[Bt["ek2"]], scale=-1.0)
            ACT(dl[:, tb * 8:(tb + 1) * 8].rearrange("p (c o) -> p c o", o=1), b3[:, :, 63:64], AF.Exp,
                R=[Bt["bcs"]], PW=[Bt["dl"]])
            if tb >= 2:
                ACT(dl[:, 32 + (tb - 2) * 8: 32 + (tb - 1) * 8].rearrange("p (c o) -> p c o", o=1), b3[:, :, 31:32],
                    AF.Exp, R=[Bt["bcs"]], PW=[Bt["dl"]])
            TT(khT, omf, ek2, ALU.mult, R=[Bt["omf"], Bt["ek2"]], W=[Bt["khT"]])
            if tb >= 2:
                o0 = (tb - 2) * 512
                TT(ktT[:, o0:o0 + 512], omf, ek, ALU.mult, R=[Bt["omf"], Bt["ek"]], PW=[Bt["ktT"]])
                TT(bown[:, o0:o0 + 512].rearrange("p (c t) -> p c t", t=64), b3,
                   b3[:, :, 31:32].to_broadcast([128, 8, 64]), ALU.subtract, R=[Bt["bcs"]], PW=[Bt["bown"]])
            bt_ = nb()
            for c in range(8):
                TR(bankbf(bt_)[0:64, c * 128:(c + 1) * 128], khT[:, c * 64:(c + 1) * 64], R=[Bt["khT"]], PW=[PB[bt_]])
            CP("act", kh_tm[0:64, tb * 1024:(tb + 1) * 1024], bankbf(bt_)[0:64, :], R=[PB[bt_]], PW=[Bt["kh_tm"]])
            bt2 = nb()
            for c in range(8):
                TR(bankbf(bt2)[0:64, c * 128:(c + 1) * 128], viT[:, c * 64:(c + 1) * 64], R=[Bt["viT"]], PW=[PB[bt2]])
            CP("dve", v_tm[0:64, tb * 1024:(tb + 1) * 1024], bankbf(bt2)[0:64, :], R=[PB[bt2]], PW=[Bt["v_tm"]])
        w_prefetch()
        w1, wb1 = w_next()
        S.rotate(Bt["qtT"]); S.rotate(Bt["gsil"])
        for tb in range(2):
            t0 = 1024 + tb * 512
            bq = nb()
            proj_fm(w1, wb1, 0, 256, 16, hsrc, t0, 512, bq)
            ACT(qs, bank(bq), AF.Silu, R=[PB[bq]], W=[Bt["qs"]])
            bg = nb()
            proj_fm(w1, wb1, 1, 256, 16, hsrc, t0, 512, bg)
            ACT(t1, bank(bg), AF.Silu, R=[PB[bg]], W=[Bt["t1"]])
            TS(gsil[:, tb * 512:(tb + 1) * 512], t1, pp[:, PP_OG + h:PP_OG + h + 1], None, ALU.mult, ALU.bypass,
               R=[Bt["t1"], Bpp], PW=[Bt["gsil"]])
            ACT(eqb, bown[:, tb * 512:(tb + 1) * 512], AF.Exp, R=[Bt["bown"]], W=[Bt["eqb"]])
            TT(qtT[:, tb * 512:(tb + 1) * 512], qs, eqb, ALU.mult, R=[Bt["qs"], Bt["eqb"]], PW=[Bt["qtT"]])
        w_prefetch()
        S.op("dve", lambda e: e.memset(Sst, 0.0), writes=[Bt["Sst"]])
        S.rotate(Bt["Sbf"])
        for c8 in range(4):
            bs = nb2()
            for cc in range(8):
                c = c8 * 8 + cc
                tgt = PD[bs // 2][:, cc * 128:(cc + 1) * 128]
                pbk = PB[bs + (cc // 4)]
                MM(tgt, kh_tm[0:64, c * 128:(c + 1) * 128], v_tm[0:64, c * 128:(c + 1) * 128], True, True,
                   R=[Bt["kh_tm"], Bt["v_tm"]], PW=[pbk])
            for cc in range(8):
                c = c8 * 8 + cc
                tgt = PD[bs // 2][:, cc * 128:(cc + 1) * 128]
                pbk = PB[bs + (cc // 4)]
                if c >= 16:
                    ACT(Sbf[:, (c - 16) * 128:(c - 15) * 128], Sst, AF.Copy, R=[Bt["Sst"], Bt["dl"]], PW=[Bt["Sbf"]],
                        scale=dl[:, 32 + c - 16: 33 + c - 16])
                STT(Sst, Sst, dl[:, c:c + 1], tgt, ALU.mult, ALU.add, R=[Bt["Sst"], Bt["dl"], pbk], W=[Bt["Sst"]])
        S.rotate(Bt["smT"])
        for half in range(2):
            bsc = nb()
            for cc in range(8):
                c = half * 8 + cc
                MM(bank(bsc)[0:64, cc * 64:(cc + 1) * 64], ktT[:, c * 64:(c + 1) * 64], qtT[:, c * 64:(c + 1) * 64],
                   True, True, R=[Bt["ktT"], Bt["qtT"]], PW=[PB[bsc]])
            TT(smT[0:64, half * 512:(half + 1) * 512], bank(bsc)[0:64, :], cmask[:], ALU.mult,
               R=[PB[bsc], Bconst], PW=[Bt["smT"]])
        for half in range(2):
            bo = nb()
            for cc in range(8):
                c = half * 8 + cc
                tgt = bank(bo)[:, cc * 64:(cc + 1) * 64]
                MM(tgt, v_tm[0:64, (16 + c) * 128:(17 + c) * 128], smT[0:64, c * 64:(c + 1) * 64], True, False,
                   R=[Bt["v_tm"], Bt["smT"]], PW=[PB[bo]])
                MM(tgt, Sbf[:, c * 128:(c + 1) * 128], qtT[:, c * 64:(c + 1) * 64], False, True,
                   R=[Bt["Sbf"], Bt["qtT"]], PW=[PB[bo]])
            ACT(osq, bank(bo), AF.Square, R=[PB[bo]], W=[Bt["osq"]])
            bn_ = nb()
            MM(bank(bn_), ones[:], osq, True, True, R=[Bconst, Bt["osq"]], PW=[PB[bn_]])
            ACT(rstd, bank(bn_), AF.Ln, R=[PB[bn_]], W=[Bt["rstd"]], scale=1.0 / 128, bias=epsc)
            ACT(rstd, rstd, AF.Exp, R=[Bt["rstd"]], W=[Bt["rstd"]], scale=-0.5)
            TT(t1, bank(bo), rstd, ALU.mult, R=[PB[bo], Bt["rstd"]], W=[Bt["t1"]])
            TT(yaT[:, h * 1024 + half * 512: h * 1024 + (half + 1) * 512], t1, gsil[:, half * 512:(half + 1) * 512],
               ALU.mult, R=[Bt["t1"], Bt["gsil"]], PW=[ByaT])
    if stop == "A":
        return finish([("yaT", yaT, [128, 8192], BF16, [ByaT]), ("hTo", hTo[:], [128, 16384], BF16, [BhTo]),
                       ("hTp", hTp, [128, 16384], BF16, [BhTp])])
    S.barrier()

    biasT = R4[:, 0:5120]
    BbiasT = Buf("biasT")
    S.dma("sp", biasT, biasd, slot="c3", writes=[BbiasT])
    for h in range(8):
        TT(biasT[:, h * 640:(h + 1) * 640], biasT[:, h * 640:(h + 1) * 640], smask, ALU.add,
           R=[BbiasT, Bcst], W=[BbiasT])
    qnT = r1f(0, 512).bitcast(BF16)
    knT = r1f(512, 768).bitcast(BF16)
    vT = r1f(1280, 768).bitcast(BF16)
    vtm = r1f(2048, 768).bitcast(BF16)
    sqb = r1f(2816, 256).bitcast(BF16)
    rsb = r1f(3072, 512)
    tmpS = r1f(3584, 640)
    Pb = r1f(4224, 320).bitcast(BF16)
    rinv = r1f(4544, 512)
    Bb = {n: Buf(n) for n in ["qnT", "knT", "vT", "vtm", "sqb", "rsb", "tmpS", "Pb", "rinv"]}
    scale = 128 ** -0.5

    def headnorm(bz, n, gcol, dst, bdst):
        ACT(sqb[:, 0:n], bank(bz)[:, 0:n], AF.Square, R=[PB[bz]], W=[Bb["sqb"]])
        bn_ = nb()
        MM(bank(bn_)[:, 0:n], ones[:], sqb[:, 0:n], True, True, R=[Bconst, Bb["sqb"]], PW=[PB[bn_]])
        ACT(rsb[:, 0:n], bank(bn_)[:, 0:n], AF.Ln, R=[PB[bn_]], W=[Bb["rsb"]], scale=1.0 / 128, bias=epsc)
        ACT(rsb[:, 0:n], rsb[:, 0:n], AF.Exp, R=[Bb["rsb"]], W=[Bb["rsb"]], scale=-0.5)
        STT(dst, bank(bz)[:, 0:n], gcol, rsb[:, 0:n], ALU.mult, ALU.mult, R=[PB[bz], Bb["rsb"], Bpp], PW=[bdst])

    for h in range(8):
        wk, wkb = w_next()
        S.rotate(Bb["knT"]); S.rotate(Bb["vT"]); S.rotate(Bb["vtm"]); S.rotate(Bb["qnT"])
        for tb in range(3):
            t0 = 512 + tb * 512
            bk = nb()
            proj_fm(wk, wkb, 0, 256, 16, hsrc, t0, 512, bk)
            headnorm(bk, 512, pp[:, PP_KG:PP_KG + 1], knT[:, tb * 512:(tb + 1) * 512], Bb["knT"])
            bv = nb()
            proj_fm(wk, wkb, 1, 256, 16, hsrc, t0, 512, bv)
            CP("act", vT[:, tb * 512:(tb + 1) * 512], bank(bv), R=[PB[bv]], PW=[Bb["vT"]])
        w_prefetch()
        wq, wqb = w_next()
        for tb in range(2):
            bq = nb()
            proj_fm(wq, wqb, 0, 128, 16, hsrc, 1024 + tb * 512, 512, bq)
            headnorm(bq, 512, pp[:, PP_QG:PP_QG + 1], qnT[:, tb * 512:(tb + 1) * 512], Bb["qnT"])
        w_prefetch()
        for half in range(2):
            bt_ = nb()
            nblk = 8 if half == 0 else 4
            for j in range(nblk):
                kb = half * 8 + j
                TR(bankbf(bt_)[:, j * 128:(j + 1) * 128], vT[:, kb * 128:(kb + 1) * 128], R=[Bb["vT"]], PW=[PB[bt_]])
            CP(alt(), vtm[:, half * 1024: half * 1024 + nblk * 128], bankbf(bt_)[:, 0:nblk * 128],
               R=[PB[bt_]], PW=[Bb["vtm"]])
        for half in range(2):
            bo, br = (0, 1) if half == 0 else (4, 5)
            S.rotate(PB[bo]); S.rotate(PB[br])
            sc_banks = [2, 4, 6, 2] if half == 0 else [6, 0, 2, 6]
            for qq in range(4):
                i = half * 4 + qq
                b2 = sc_banks[qq]
                S.rotate(PB[b2]); S.rotate(PB[b2 + 1])
                stile = PD[b2 // 2]
                for r in range(5):
                    kb = i + 4 - r
                    MM(stile[:, r * 128:(r + 1) * 128], knT[:, kb * 128:(kb + 1) * 128], qnT[:, i * 128:(i + 1) * 128],
                       True, True, R=[Bb["knT"], Bb["qnT"]], PW=[PB[b2 + (r // 4)]])
                STT(tmpS, stile[:, 0:640], scale, biasT[:, h * 640:(h + 1) * 640], ALU.mult, ALU.add,
                    R=[PB[b2], PB[b2 + 1], BbiasT], W=[Bb["tmpS"]])
                npre = max(0, 4 - i)
                if npre < 5:
                    ACT(Pb[:, 0:(5 - npre) * 128], tmpS[:, 0:(5 - npre) * 128], AF.Exp, R=[Bb["tmpS"]], W=[Bb["Pb"]])
                if npre > 0:
                    ACT(Pb[:, (5 - npre) * 128:640], tmpS[:, (5 - npre) * 128:640], AF.Exp,
                        R=[Bb["tmpS"], Bpp], PW=[Bb["Pb"]] if npre < 5 else (), W=() if npre < 5 else [Bb["Pb"]],
                        bias=pp[:, PP_PM:PP_PM + 1])
                for r in range(5):
                    kb = i + 4 - r
                    MM(bank(bo)[:, qq * 128:(qq + 1) * 128], vtm[:, kb * 128:(kb + 1) * 128], Pb[:, r * 128:(r + 1) * 128],
                       r == 0, r == 4, R=[Bb["vtm"], Bb["Pb"]], PW=[PB[bo]])
                for r in range(5):
                    MM(bank(br)[:, qq * 128:(qq + 1) * 128], ones[:], Pb[:, r * 128:(r + 1) * 128],
                       r == 0, r == 4, R=[Bconst, Bb["Pb"]], PW=[PB[br]])
            S.op("dve", lambda e, br=br: e.reciprocal(out=rinv, in_=bank(br)), reads=[PB[br]], writes=[Bb["rinv"]])
            TT(ybT[:, h * 1024 + half * 512: h * 1024 + (half + 1) * 512], bank(bo), rinv, ALU.mult,
               R=[PB[bo], Bb["rinv"]], PW=[BybT])
    if stop == "B":
        return finish([("yaT", yaT, [128, 8192], BF16, [ByaT]), ("ybT", ybT, [128, 8192], BF16, [BybT])])
    S.barrier()

    mT = R4.bitcast(BF16)
    BmT = Buf("mT")
    hb = r1f(0, 32)
    tha = r1f(64, 512)
    thb = r1f(576, 512)
    ma = r1f(1088, 512)
    mb = r1f(1600, 512)
    Bc = {n: Buf(n) for n in ["hb", "tha", "thb", "ma", "mb"]}
    TS(hb, pp[:, PP_BGA:PP_BGA + 32], 0.5, None, ALU.mult, ALU.bypass, R=[Bpp], W=[Bc["hb"]])

    def ysrc(base, bb):
        def f(kc, t0, n):
            return base[:, kc * 1024 + t0: kc * 1024 + t0 + n], bb
        return f

    def hown(kc, t0, n):
        return hTo[:, kc * 1024 + t0: kc * 1024 + t0 + n], BhTo

    for c in range(16):
        wg, wgb = w_next()
        res = {}
        for half in range(2):
            b_ga = nb()
            proj_fm(wg, wgb, 0, 256, 16, hown, half * 512, 512, b_ga)
            b_gb = nb()
            proj_fm(wg, wgb, 1, 256, 16, hown, half * 512, 512, b_gb)
            res[half] = (b_ga, b_gb)
        w_prefetch()
        wp, wpb = w_next()
        for half in range(2):
            b_ga, b_gb = res[half]
            b_pa = nb()
            proj_fm(wp, wpb, 0, 256, 8, ysrc(yaT, ByaT), half * 512, 512, b_pa)
            b_pb = nb()
            proj_fm(wp, wpb, 1, 256, 8, ysrc(ybT, BybT), half * 512, 512, b_pb)
            ACT(tha, bank(b_ga), AF.Tanh, R=[PB[b_ga], Bc["hb"]], W=[Bc["tha"]], scale=0.5, bias=hb[:, c:c + 1])
            ACT(thb, bank(b_gb), AF.Tanh, R=[PB[b_gb], Bc["hb"]], W=[Bc["thb"]], scale=0.5, bias=hb[:, 16 + c:17 + c])
            STT(ma, tha, 1.0, bank(b_pa), ALU.add, ALU.mult, R=[Bc["tha"], PB[b_pa]], W=[Bc["ma"]])
            STT(mb, thb, 1.0, bank(b_pb), ALU.add, ALU.mult, R=[Bc["thb"], PB[b_pb]], W=[Bc["mb"]])
            TT(mT[:, c * 1024 + half * 512: c * 1024 + (half + 1) * 512], ma, mb, ALU.add,
               R=[Bc["ma"], Bc["mb"]], PW=[BmT])
        w_prefetch()
    if stop == "C":
        return finish([("mT", mT, [128, 16384], BF16, [BmT])])
    S.barrier()

    X = R1
    for r in range(8):
        S.dma("sp", X[:, r * 2048:(r + 1) * 2048], xo[r * 128:(r + 1) * 128, :], slot=f"xd{r}", writes=[BX[r]])
    for g in range(8):
        wd, wdb = w_next()
        for r in range(8):
            b = nb()
            for kc in range(16):
                MM(bank(b)[:, 0:256], mT[:, kc * 1024 + r * 128: kc * 1024 + (r + 1) * 128],
                   wd[:, kc * 256:(kc + 1) * 256], kc == 0, kc == 15, R=[BmT, wdb], PW=[PB[b]])
            xs = X[:, r * 2048 + g * 256: r * 2048 + (g + 1) * 256]
            STT(xs, bank(b)[:, 0:256], 0.5, xs, ALU.mult, ALU.add, R=[PB[b], BX[r]], W=[BX[r]])
        w_prefetch()
    if stop == "D":
        return finish([("X1", X[:], [128, 16384], F32, BX)])
    S.barrier()

    S.rotate(BhTo)
    norm_phase([("X", r) for r in range(8)], 1, None)
    S.barrier()

    actb = R3
    Bact = Buf("act")
    sil = R4[:, 0:512]
    Bsil = Buf("sil")
    last = []
    for g in range(NFC // FG):
        S.rotate(Bact)
        for j in range(FG):
            wf, wfb = w_next()
            for half in range(2):
                bg = nb()
                proj_fm(wf, wfb, 0, 256, 16, hown, half * 512, 512, bg)
                bu = nb()
                proj_fm(wf, wfb, 1, 256, 16, hown, half * 512, 512, bu)
                ACT(sil, bank(bg), AF.Silu, R=[PB[bg]], W=[Bsil])
                TT(actb[:, j * 1024 + half * 512: j * 1024 + (half + 1) * 512], sil, bank(bu), ALU.mult,
                   R=[Bsil, PB[bu]], PW=[Bact])
            w_prefetch()
        for cg in range(8):
            w2, w2b = w_next()
            for r in range(8):
                b = nb()
                for j in range(FG):
                    MM(bank(b)[:, 0:256], actb[:, j * 1024 + r * 128: j * 1024 + (r + 1) * 128],
                       w2[:, j * 256:(j + 1) * 256], j == 0, j == FG - 1, R=[Bact, w2b], PW=[PB[b]])
                xs = X[:, r * 2048 + cg * 256: r * 2048 + (cg + 1) * 256]
                TT(xs, bank(b)[:, 0:256], xs, ALU.add, R=[PB[b], BX[r]], W=[BX[r]])
            w_prefetch()
    for r in range(8):
        last.append(S.dma("sp", yout[r * 128:(r + 1) * 128, :], X[:, r * 2048:(r + 1) * 2048], slot=f"o{r}",
                          reads=[BX[r]]))
    S.emit(final_waits=last)
    return nc, []


D_A = 1024
D_B = 1024


def _wtile(wcols, kcn):
    K, C = wcols.shape
    assert K == kcn * 128
    return np.ascontiguousarray(wcols.reshape(kcn, 128, C).transpose(1, 0, 2).reshape(128, kcn * C))


def _prep_shared(inp):
    w_in = np.asarray(inp["w_in"], np.float32)[0]
    c_q, c_f, c_i, c_g = 0, D_A, 2 * D_A, 3 * D_A
    c_qb, c_kb, c_vb = 4 * D_A, 4 * D_A + D_B, 4 * D_A + 2 * D_B
    c_ga, c_gb = 4 * D_A + 3 * D_B, 4 * D_A + 3 * D_B + D
    sl = lambda c0, h: w_in[:, c0 + h * 128: c0 + (h + 1) * 128]
    wA = np.stack([_wtile(np.concatenate(p, axis=1), 16) for h in range(8)
                   for p in ((sl(c_f, h), sl(c_i, h)), (sl(c_q, h), sl(c_g, h)))])
    wBkv = np.stack([_wtile(np.concatenate((sl(c_kb, h), sl(c_vb, h)), axis=1), 16) for h in range(8)])
    wBq = np.stack([_wtile(sl(c_qb, h), 16) for h in range(8)])
    wCg = np.stack([_wtile(np.concatenate((sl(c_ga, c), sl(c_gb, c)), axis=1), 16) for c in range(16)])
    wpa = np.asarray(inp["w_proj_a"], np.float32)[0]
    wpb = np.asarray(inp["w_proj_b"], np.float32)[0]
    wCp = np.stack([_wtile(np.concatenate((wpa[:, c * 128:(c + 1) * 128], wpb[:, c * 128:(c + 1) * 128]), axis=1), 8)
                    for c in range(16)])
    wo = np.asarray(inp["w_out"], np.float32)[0]
    wD = np.stack([_wtile(wo[:, g * 256:(g + 1) * 256], 16) for g in range(8)])
    wfi = np.asarray(inp["w_ffn_in"], np.float32)[0]
    wF1 = np.stack([_wtile(np.concatenate((wfi[:, j * 128:(j + 1) * 128], wfi[:, DFF + j * 128: DFF + (j + 1) * 128]),
                                          axis=1), 16) for j in range(NFC)])
    wfo = np.asarray(inp["w_ffn_out"], np.float32)[0]
    wF2 = np.stack([np.stack([_wtile(wfo[g * FG * 128:(g + 1) * FG * 128, cg * 256:(cg + 1) * 256], FG)
                              for cg in range(8)]) for g in range(NFC // FG)])
    pp = np.zeros((128, NPP), np.float32)
    lbl = np.asarray(inp["hgrn_lb_logits"], np.float32)
    pp[:, PP_L0:PP_L0 + 8] = lbl[0].reshape(8, 128).T
    pp[:, PP_L1:PP_L1 + 8] = lbl[1].reshape(8, 128).T
    pp[:, PP_OG:PP_OG + 8] = np.asarray(inp["hgrn_out_gain"], np.float32)[0].reshape(8, 128).T
    bg = np.asarray(inp["b_gate"], np.float32)[0]
    pp[:, PP_BGA:PP_BGA + 16] = bg[:D].reshape(16, 128).T
    pp[:, PP_BGB:PP_BGB + 16] = bg[D:].reshape(16, 128).T
    pp[:, PP_QG] = np.asarray(inp["q_gain"], np.float32)[0]
    pp[:, PP_KG] = np.asarray(inp["k_gain"], np.float32)[0]
    pp[:, PP_EPS] = 1e-6
    gbc = np.stack([np.broadcast_to(np.asarray(inp["norm_mix"], np.float32)[0][None, :], (128, D)),
                    np.broadcast_to(np.asarray(inp["norm_ffn"], np.float32)[0][None, :], (128, D))]).copy()
    cst = np.zeros((128, 128 + 128 + 512 + 512 + 640), np.float32)
    cst[:, 0:128] = np.eye(128, dtype=np.float32)
    cst[:, 128:256] = 1.0
    cm = np.triu(np.ones((64, 64), np.float32))
    cst[0:64, 256:768] = np.tile(cm, (1, 8))
    sm = np.ones((128, 512), np.float32)
    sm[:, 0::64] = 0.0
    cst[:, 768:1280] = sm
    kl = np.arange(128)[:, None, None]
    r = np.arange(5)[None, :, None]
    ql = np.arange(128)[None, None, :]
    dist = (ql + 128 * r) - kl
    kchunk_rel = (2 * 0 - 2 * r) + (kl // 64)
    qchunk_rel = ql // 64
    dchunk = qchunk_rel - kchunk_rel
    valid = (dchunk >= 0) & (dchunk <= 8)
    smask = np.where(valid, 0.0, NEG).astype(np.float32)
    cst[:, 1280:1920] = smask.reshape(128, 640)
    ridx = np.clip(dist, -63, 127) + 63
    rb = np.asarray(inp["rel_bias"], np.float32)[0]
    biasT = np.concatenate([rb[hh][ridx].reshape(128, 640) for hh in range(8)], axis=1)
    return dict(wA=wA, wBkv=wBkv, wBq=wBq, wCg=wCg, wCp=wCp, wD=wD, wF1=wF1, wF2=wF2, gbc=gbc, cst=cst,
                biasT=np.ascontiguousarray(biasT, np.float32)), pp


_CACHE = {}


def kernel(**inputs):
    x = np.asarray(inputs["x"], np.float32)
    shared, pp = _prep_shared(inputs)
    in_maps = []
    for c in range(8):
        b, half = c // 2, c % 2
        m = dict(shared)
        m["xo"] = np.ascontiguousarray(x[b, half * T:(half + 1) * T])
        m["xp"] = np.ascontiguousarray(x[b, 0:T]) if half == 1 else np.zeros((T, D), np.float32)
        ppc = pp.copy()
        ppc[:, PP_PM] = 0.0 if half == 1 else NEG
        m["pp"] = ppc
        in_maps.append(m)
    if "nc" not in _CACHE:
        _CACHE["nc"] = build(None)[0]
    res = run_bass_kernel_spmd(_CACHE["nc"], in_maps, core_ids=list(range(8)))
    out = np.zeros((4, 2048, D), np.float32)
    for c in range(8):
        b, half = c // 2, c % 2
        out[b, half * T:(half + 1) * T] = res.results[c]["y"]
    return out
```

```python
import contextlib
import os
import numpy as np
import concourse.bass as bass
import concourse.mybir as mybir
from concourse.bass_utils import run_bass_kernel_spmd

F32 = mybir.dt.float32
BF16 = mybir.dt.bfloat16
AF = mybir.ActivationFunctionType
ALU = mybir.AluOpType
AX = mybir.AxisListType

ENGS = ("pe", "act", "dve", "pool", "sp")

D = 2048
T = 1024
NPOS = 2048
DFF = 5632
NFC = 44
FG = 11
NEG = -30000.0


class Buf:
    __slots__ = ("name", "writers", "readers", "prev")

    def __init__(self, name=""):
        self.name = name
        self.writers = []
        self.readers = []
        self.prev = []


class Op:
    __slots__ = ("eng", "fn", "deps", "idx", "is_dma", "slot", "val", "signaled")

    def __init__(self, eng, fn, is_dma=False, slot=None):
        self.eng = eng
        self.fn = fn
        self.deps = []
        self.idx = -1
        self.is_dma = is_dma
        self.slot = slot
        self.val = 0
        self.signaled = False


class Sched:
    def __init__(self, nc):
        self.nc = nc
        self.q = {e: [] for e in ENGS}
        self.slot_count = {}
        self.slot_last = {}
        self.bar = {e: [] for e in ENGS}

    def _hazards(self, op, reads, writes, pwrites):
        deps = op.deps
        for b in reads:
            deps.extend(b.writers)
            b.readers.append(op)
        for b in writes:
            deps.extend(b.readers)
            deps.extend(b.writers)
            deps.extend(b.prev)
            b.prev = []
            b.writers = [op]
            b.readers = []
        for b in pwrites:
            deps.extend(b.prev)
            deps.extend(b.readers)
            b.writers.append(op)

    def _add(self, o, eng, reads, writes, pwrites, deps):
        o.deps.extend(deps)
        if self.bar[eng]:
            o.deps.extend(self.bar[eng])
            self.bar[eng] = []
        self._hazards(o, reads, writes, pwrites)
        o.idx = len(self.q[eng])
        self.q[eng].append(o)
        return o

    def op(self, eng, fn, reads=(), writes=(), pwrites=(), deps=()):
        return self._add(Op(eng, fn), eng, reads, writes, pwrites, deps)

    def dma(self, eng, out, in_, slot, reads=(), writes=(), pwrites=(), deps=()):
        def fn(e):
            return e.dma_start(out=out, in_=in_)
        o = Op(eng, fn, is_dma=True, slot=slot)
        c = self.slot_count.get(slot, 0) + 1
        self.slot_count[slot] = c
        o.val = 16 * c
        self.slot_last[slot] = o
        return self._add(o, eng, reads, writes, pwrites, deps)

    @staticmethod
    def rotate(b):
        b.prev = b.prev + b.readers + b.writers
        b.readers = []
        b.writers = []

    def barrier(self):
        lasts = [self.q[e][-1] for e in ENGS if self.q[e] and not self.q[e][-1].is_dma]
        lasts = []
        for e in ENGS:
            for o in reversed(self.q[e]):
                if not o.is_dma:
                    lasts.append(o)
                    break
        lasts.extend(self.slot_last.values())
        for e in ENGS:
            if e != "pool":
                self.bar[e] = list(lasts)

    def emit(self, final_waits=()):
        nc = self.nc
        for e in ENGS:
            for o in self.q[e]:
                best = {}
                for d in o.deps:
                    if d is o:
                        continue
                    if d.is_dma:
                        key = ("slot", d.slot)
                        v = d.val
                    else:
                        if d.eng == e:
                            if e == "pe":
                                continue
                            if o.idx - d.idx > 2:
                                continue
                        key = ("eng", d.eng)
                        v = d.idx
                    if key not in best or v > best[key][0]:
                        best[key] = (v, d)
                o.deps = [d for (_, d) in best.values()]
                for d in o.deps:
                    d.signaled = True
        for o in final_waits:
            o.signaled = True
        for e in ENGS:
            c = 0
            for o in self.q[e]:
                if o.is_dma:
                    continue
                if o.signaled:
                    c += 1
                    o.val = c
        slots = sorted(self.slot_count.keys())
        with contextlib.ExitStack() as st:
            esem = {e: st.enter_context(nc.semaphore("s_" + e)) for e in ENGS}
            ssem = {s: st.enter_context(nc.semaphore("d_" + str(s))) for s in slots}
            block = st.enter_context(nc.Block())
            deco = {"pe": block.tensor, "act": block.scalar, "dve": block.vector,
                    "pool": block.gpsimd, "sp": block.sync}

            def make(e):
                def body(eng):
                    waited = {}
                    for o in self.q[e]:
                        for d in o.deps:
                            if d.is_dma:
                                sem = ssem[d.slot]
                                key = ("slot", d.slot)
                            else:
                                sem = esem[d.eng]
                                key = ("eng", d.eng)
                            if waited.get(key, 0) >= d.val:
                                continue
                            eng.wait_ge(sem, d.val)
                            waited[key] = d.val
                        ins = o.fn(eng)
                        if o.is_dma:
                            ins.then_inc(ssem[o.slot], 16)
                        elif o.signaled:
                            ins.then_inc(esem[e], 1)
                    if e == "sp":
                        for o in final_waits:
                            if o.is_dma:
                                eng.wait_ge(ssem[o.slot], o.val)
                            else:
                                eng.wait_ge(esem[o.eng], o.val)
                return body

            for e in ENGS:
                deco[e](make(e))


NPP = 64
PP_L0, PP_L1, PP_OG, PP_BGA, PP_BGB, PP_QG, PP_KG, PP_PM, PP_EPS = 0, 8, 16, 24, 40, 56, 57, 58, 59


def build(stop=None):
    nc = bass.Bass("TRN2", target_bir_lowering=False)
    dt = nc.dram_tensor
    xo = dt("xo", [T, D], F32, kind="ExternalInput").ap()
    xp = dt("xp", [T, D], F32, kind="ExternalInput").ap()
    ppd = dt("pp", [128, NPP], F32, kind="ExternalInput").ap()
    cst = dt("cst", [128, 128 + 128 + 512 + 512 + 640], F32, kind="ExternalInput").ap()
    gbc = dt("gbc", [2, 128, D], F32, kind="ExternalInput").ap()
    biasd = dt("biasT", [128, 8 * 640], F32, kind="ExternalInput").ap()
    wA = dt("wA", [16, 128, 16 * 256], F32, kind="ExternalInput").ap()
    wBkv = dt("wBkv", [8, 128, 16 * 256], F32, kind="ExternalInput").ap()
    wBq = dt("wBq", [8, 128, 16 * 128], F32, kind="ExternalInput").ap()
    wCg = dt("wCg", [16, 128, 16 * 256], F32, kind="ExternalInput").ap()
    wCp = dt("wCp", [16, 128, 8 * 256], F32, kind="ExternalInput").ap()
    wD = dt("wD", [8, 128, 16 * 256], F32, kind="ExternalInput").ap()
    wF1 = dt("wF1", [NFC, 128, 16 * 256], F32, kind="ExternalInput").ap()
    wF2 = dt("wF2", [NFC // FG, 8, 128, FG * 256], F32, kind="ExternalInput").ap()
    yout = dt("y", [T, D], F32, kind="ExternalOutput").ap()
    dbg_out = {}

    def finish(dumps):
        last = []
        for i, (name, ap, shape, dty, bufs) in enumerate(dumps):
            dd = dt("dbg_" + name, shape, dty, kind="ExternalOutput").ap()
            last.append(S.dma("sp", dd, ap, slot=f"dbg{i}", reads=bufs))
        S.emit(final_waits=last)
        return nc, [d[0] for d in dumps]

    S = Sched(nc)
    A_ = nc.alloc_sbuf_tensor

    R1 = A_("R1", [128, 16384], F32)
    R2 = A_("R2", [128, 16384], BF16)
    R3 = A_("R3", [128, 16384], BF16)
    R4 = A_("R4", [128, 8192], F32)
    WS = [A_(f"WS{i}", [128, 4096], BF16) for i in range(2)]
    WB = [Buf(f"WS{i}") for i in range(2)]
    pp = A_("pp_sb", [128, NPP], F32)
    ppx = A_("ppx", [128, 32], F32)
    cstf = A_("cstf", [128, 128 + 128 + 512 + 512 + 640], F32)
    ident = A_("ident", [128, 128], BF16)
    ones = A_("ones", [128, 128], BF16)
    cmask = A_("cmask", [64, 512], F32)
    Bpp, Bppx, Bcst, Bconst = Buf(), Buf(), Buf(), Buf()
    scanm = cstf[:, 768:1280]
    smask = cstf[:, 1280:1920]

    PD = [nc.alloc_psum_tensor(f"PD{i}", [128, 1024], F32) for i in range(4)]
    PB = [Buf(f"bank{i}") for i in range(8)]

    def bank(b):
        return PD[b // 2][:, (b % 2) * 512:(b % 2) * 512 + 512]

    def bankbf(b):
        return bank(b).bitcast(BF16)

    state = {"bank": 0, "w": 0}

    def nb():
        b = state["bank"]
        state["bank"] = (b + 1) % 8
        S.rotate(PB[b])
        return b

    def nb2():
        if state["bank"] % 2:
            state["bank"] = (state["bank"] + 1) % 8
        b = state["bank"]
        state["bank"] = (b + 2) % 8
        S.rotate(PB[b])
        S.rotate(PB[b + 1])
        return b

    wlist = []
    for h in range(8):
        wlist.append((wA[2 * h], 4096))
        wlist.append((wA[2 * h + 1], 4096))
    for h in range(8):
        wlist.append((wBkv[h], 4096))
        wlist.append((wBq[h], 2048))
    for c in range(16):
        wlist.append((wCg[c], 4096))
        wlist.append((wCp[c], 2048))
    for g in range(8):
        wlist.append((wD[g], 4096))
    for g in range(NFC // FG):
        for j in range(FG):
            wlist.append((wF1[g * FG + j], 4096))
        for cg in range(8):
            wlist.append((wF2[g, cg], FG * 256))
    wstate = {"issued": 0, "taken": 0}

    def w_issue():
        i = wstate["issued"]
        if i >= len(wlist):
            return
        src, n = wlist[i]
        s = i % 2
        S.dma("pool", WS[s][:, 0:n], src, slot=f"w{s}", writes=[WB[s]])
        wstate["issued"] = i + 1

    def w_next():
        i = wstate["taken"]
        while wstate["issued"] <= i:
            w_issue()
        wstate["taken"] = i + 1
        return WS[i % 2], WB[i % 2]

    def w_prefetch():
        while wstate["issued"] < min(len(wlist), wstate["taken"] + 2):
            w_issue()

    def ACT(out, in_, func, R=(), W=(), PW=(), scale=1.0, bias=None, accum_out=None):
        kw = {}
        if bias is not None:
            kw["bias"] = bias
        if accum_out is not None:
            kw["accum_out"] = accum_out
        return S.op("act", lambda e: e.activation(out=out, in_=in_, func=func, scale=scale, **kw),
                    reads=R, writes=W, pwrites=PW)

    def TS(out, in0, s1, s2, op0, op1, R=(), W=(), PW=(), eng="dve"):
        if s2 is None:
            return S.op(eng, lambda e: e.tensor_single_scalar(out=out, in_=in0, scalar=s1, op=op0),
                        reads=R, writes=W, pwrites=PW)
        return S.op(eng, lambda e: e.tensor_scalar(out=out, in0=in0, scalar1=s1, scalar2=s2, op0=op0, op1=op1),
                    reads=R, writes=W, pwrites=PW)

    def TT(out, in0, in1, op, R=(), W=(), PW=(), eng="dve"):
        return S.op(eng, lambda e: e.tensor_tensor(out=out, in0=in0, in1=in1, op=op),
                    reads=R, writes=W, pwrites=PW)

    def STT(out, in0, scalar, in1, op0, op1, R=(), W=(), PW=()):
        return S.op("dve", lambda e: e.scalar_tensor_tensor(out=out, in0=in0, scalar=scalar, in1=in1,
                                                            op0=op0, op1=op1),
                    reads=R, writes=W, pwrites=PW)

    def CP(eng, out, in_, R=(), W=(), PW=()):
        if eng == "act":
            return ACT(out, in_, AF.Copy, R=R, W=W, PW=PW)
        return S.op(eng, lambda e: e.tensor_copy(out=out, in_=in_), reads=R, writes=W, pwrites=PW)

    def MM(out, lhsT, rhs, start, stop, R=(), PW=()):
        return S.op("pe", lambda e: e.matmul(out, lhsT, rhs, start=start, stop=stop), reads=R, pwrites=PW)

    def TR(out, in_, R=(), PW=()):
        return S.op("pe", lambda e: e.transpose(out, in_, ident[:]), reads=list(R) + [Bconst], pwrites=PW)

    flip = {"i": 0}

    def alt():
        flip["i"] ^= 1
        return "act" if flip["i"] else "dve"

    S.dma("sp", pp[:], ppd, slot="c0", writes=[Bpp])
    S.dma("sp", cstf[:], cst, slot="c1", writes=[Bcst])
    CP("dve", ident[:], cstf[:, 0:128], R=[Bcst], W=[Bconst])
    CP("dve", ones[:], cstf[:, 128:256], R=[Bcst], PW=[Bconst])
    CP("dve", cmask[:], cstf[0:64, 256:768], R=[Bcst], PW=[Bconst])
    TT(ppx[:, 0:8], pp[:, PP_L0:PP_L0 + 8], pp[:, PP_L1:PP_L1 + 8], ALU.subtract, R=[Bpp], W=[Bppx])
    ACT(ppx[:, 0:8], ppx[:, 0:8], AF.Sigmoid, R=[Bppx], W=[Bppx])
    TS(ppx[:, 8:16], ppx[:, 0:8], -1.0, 1.0, ALU.mult, ALU.add, R=[Bppx], W=[Bppx])
    TS(ppx[:, 16:24], ppx[:, 0:8], 1.0, -1.0, ALU.mult, ALU.add, R=[Bppx], W=[Bppx])
    epsc = pp[:, PP_EPS:PP_EPS + 1]

    hTp = R1[:, 0:8192].bitcast(BF16)
    hTo = R2
    BhTp, BhTo = Buf("hTp"), Buf("hTo")
    yaT = R3[:, 0:8192]
    ybT = R3[:, 8192:16384]
    ByaT, BybT = Buf("yaT"), Buf("ybT")

    def hsrc(kc, t0, n):
        if t0 < 1024:
            return hTp[:, kc * 1024 + t0: kc * 1024 + t0 + n], BhTp
        t0 -= 1024
        return hTo[:, kc * 1024 + t0: kc * 1024 + t0 + n], BhTo

    def norm_phase(tiles, gsel, tmp_base):
        XT = [R4[:, 0:2048], R4[:, 2048:4096]]
        BXT = [Buf("xt0"), Buf("xt1")]
        xns = [R4[:, 4096:5120].bitcast(BF16), R3[:, 0:2048]]
        junk = R4[:, 5120:6144].bitcast(BF16)
        gain = R4[:, 6144:8192]
        stats = [ppx[:, 24:28], ppx[:, 28:32]]
        Bxns, Bjunk, Bgain, Bstats = [Buf(), Buf()], Buf(), Buf(), [Buf(), Buf()]
        S.dma("sp", gain, gbc[gsel], slot="c2", writes=[Bgain])
        for i, (kind, r) in enumerate(tiles):
            xn, Bxn, stat, Bstat = xns[i % 2], Bxns[i % 2], stats[i % 2], Bstats[i % 2]
            if kind == "X":
                xt = R1[:, r * 2048:(r + 1) * 2048]
                bx = BX[r]
            else:
                xt = XT[i % 2]
                bx = BXT[i % 2]
                src = xp if kind == "p" else xo
                S.dma("sp", xt, src[r * 128:(r + 1) * 128, :], slot=f"x{i % 2}", writes=[bx])
            ACT(junk, xt, AF.Square, R=[bx], W=[Bjunk, Bstat], accum_out=stat[:, 0:1])
            ACT(stat[:, 1:2], stat[:, 0:1], AF.Ln, R=[Bstat], W=[Bstat], scale=1.0 / D, bias=epsc)
            ACT(stat[:, 2:3], stat[:, 1:2], AF.Exp, R=[Bstat], W=[Bstat], scale=-0.5)
            STT(xn, xt, stat[:, 2:3], gain, ALU.mult, ALU.mult, R=[bx, Bstat, Bgain], W=[Bxn])
            for half in range(2):
                b = nb()
                for j in range(8):
                    kc = half * 8 + j
                    TR(bankbf(b)[:, j * 128:(j + 1) * 128], xn[:, kc * 128:(kc + 1) * 128], R=[Bxn], PW=[PB[b]])
                if kind == "p":
                    dst, bd = hTp, BhTp
                else:
                    dst, bd = hTo, BhTo
                d3 = dst.rearrange("p (k t) -> p k t", t=1024)[:, half * 8:half * 8 + 8, r * 128:(r + 1) * 128]
                s3 = bankbf(b).rearrange("p (k t) -> p k t", t=128)
                CP(alt(), d3, s3, R=[PB[b]], PW=[bd])

    def proj_fm(wt, wb, cb, gc, kcn, src_fn, t0, n, b):
        for kc in range(kcn):
            rhs, rb = src_fn(kc, t0, n)
            MM(bank(b)[:, 0:n], wt[:, kc * gc + cb * 128: kc * gc + cb * 128 + 128], rhs,
               start=(kc == 0), stop=(kc == kcn - 1), R=[wb, rb], PW=[PB[b]])

    BX = [Buf(f"X{i}") for i in range(8)]
    w_issue()
    w_issue()
    norm_phase([("p", r) for r in range(8)] + [("o", r) for r in range(8)], 0, None)
    if stop == "0":
        return finish([("hTo", hTo[:], [128, 16384], BF16, [BhTo]), ("hTp", hTp, [128, 16384], BF16, [BhTp])])
    S.barrier()

    def r1f(off, n):
        return R1[:, 8192 + off: 8192 + off + n]

    def r4f(off, n):
        return R4[:, off: off + n]

    v_tm = r1f(0, 2048).bitcast(BF16)
    kh_tm = r1f(2048, 2048).bitcast(BF16)
    ktT = r1f(4096, 512).bitcast(BF16)
    qtT = r1f(4608, 512).bitcast(BF16)
    gsil = r1f(5120, 1024)
    Sbf = r1f(6144, 1024).bitcast(BF16)
    smT = r1f(7168, 512).bitcast(BF16)
    Sst = r1f(7680, 128)
    dl = r1f(7808, 64)
    sig = r4f(0, 512)
    omf = r4f(512, 512)
    lf = r4f(1024, 512)
    bcs = r4f(1536, 512)
    Eb = r4f(2048, 512)
    ek = r4f(2560, 512)
    ek2 = r4f(3072, 512)
    khT = r4f(3584, 256).bitcast(BF16)
    viT = r4f(3840, 256).bitcast(BF16)
    qs = r4f(4096, 512)
    eqb = r4f(4608, 512)
    osq = r4f(5120, 256).bitcast(BF16)
    rstd = r4f(5376, 512)
    t1 = r4f(5888, 512)
    bown = r4f(6400, 1024)
    rown = r4f(7424, 16)
    Bt = {n: Buf(n) for n in ["v_tm", "kh_tm", "ktT", "qtT", "gsil", "Sbf", "smT", "Sst", "dl", "sig", "omf", "lf",
                              "bcs", "Eb", "ek", "ek2", "khT", "viT", "qs", "eqb", "osq", "rstd", "t1", "bown"]}

    for h in range(8):
        lbc = ppx[:, h:h + 1]
        omlc = ppx[:, 8 + h:9 + h]
        nomlc = ppx[:, 16 + h:17 + h]
        w0, wb0 = w_next()
        S.rotate(Bt["v_tm"]); S.rotate(Bt["kh_tm"]); S.rotate(Bt["ktT"]); S.rotate(Bt["dl"]); S.rotate(Bt["bown"])
        def P_fi(tb_):
            a_ = nb()
            proj_fm(w0, wb0, 0, 256, 16, hsrc, tb_ * 512, 512, a_)
            b_ = nb()
            proj_fm(w0, wb0, 1, 256, 16, hsrc, tb_ * 512, 512, b_)
            return a_, b_

        def P_qg(tb_):
            a_ = nb()
            proj_fm(w1, wb1, 0, 256, 16, hsrc, 1024 + tb_ * 512, 512, a_)
            b_ = nb()
            proj_fm(w1, wb1, 1, 256, 16, hsrc, 1024 + tb_ * 512, 512, b_)
            return a_, b_

        cur = P_fi(0)
        for tb in range(4):
            t0 = tb * 512
            bf_, bi_ = cur
            if tb < 3:
                cur = P_fi(tb + 1)
            else:
                w_prefetch()
                w1, wb1 = w_next()
                cur = P_qg(0)
            ACT(sig, bank(bf_), AF.Sigmoid, R=[PB[bf_]], W=[Bt["sig"]])
            CP("dve", viT, bank(bi_), R=[PB[bi_]], W=[Bt["viT"]])
            TS(omf, sig, nomlc, omlc, ALU.mult, ALU.add, R=[Bt["sig"], Bppx], W=[Bt["omf"]])
            TS(sig, sig, omlc, lbc, ALU.mult, ALU.add, R=[Bt["sig"], Bppx], W=[Bt["sig"]])
            ACT(lf, sig, AF.Ln, R=[Bt["sig"]], W=[Bt["lf"]])
            S.op("dve", lambda e: e.tensor_tensor_scan(out=bcs, data0=scanm, data1=lf, initial=0.0,
                                                       op0=ALU.mult, op1=ALU.add),
                 reads=[Bt["lf"], Bcst], writes=[Bt["bcs"]])
            b3 = bcs.rearrange("p (c t) -> p c t", t=64)
            TT(Eb.rearrange("p (c t) -> p c t", t=64), b3, b3[:, :, 31:32].to_broadcast([128, 8, 64]),
               ALU.subtract, R=[Bt["bcs"]], W=[Bt["Eb"]])
            ACT(ek, Eb, AF.Exp, R=[Bt["Eb"]], W=[Bt["ek"]], scale=-1.0)
            TT(Eb.rearrange("p (c t) -> p c t", t=64), b3, b3[:, :, 63:64].to_broadcast([128, 8, 64]),
               ALU.subtract, R=[Bt["bcs"], Bt["ek"]], W=[Bt["Eb"]])
            ACT(ek2, Eb, AF.Exp, R=[Bt["Eb"]], W=[Bt["ek2"]], scale=-1.0)
            ACT(dl[:, tb * 8:(tb + 1) * 8].rearrange("p (c o) -> p c o", o=1), b3[:, :, 63:64], AF.Exp,
                R=[Bt["bcs"]], PW=[Bt["dl"]])
            if tb >= 2:
                ACT(dl[:, 32 + (tb - 2) * 8: 32 + (tb - 1) * 8].rearrange("p (c o) -> p c o", o=1), b3[:, :, 31:32],
                    AF.Exp, R=[Bt["bcs"]], PW=[Bt["dl"]])
            TT(khT, omf, ek2, ALU.mult, R=[Bt["omf"], Bt["ek2"]], W=[Bt["khT"]])
            if tb >= 2:
                o0 = (tb - 2) * 512
                TT(ktT[:, o0:o0 + 512], omf, ek, ALU.mult, R=[Bt["omf"], Bt["ek"]], PW=[Bt["ktT"]])
                TT(bown[:, o0:o0 + 512].rearrange("p (c t) -> p c t", t=64), b3,
                   b3[:, :, 31:32].to_broadcast([128, 8, 64]), ALU.subtract, R=[Bt["bcs"]], PW=[Bt["bown"]])
            bt_ = nb()
            for c in range(8):
                TR(bankbf(bt_)[0:64, c * 128:(c + 1) * 128], khT[:, c * 64:(c + 1) * 64], R=[Bt["khT"]], PW=[PB[bt_]])
            CP("act", kh_tm[0:64, tb * 1024:(tb + 1) * 1024], bankbf(bt_)[0:64, :], R=[PB[bt_]], PW=[Bt["kh_tm"]])
            bt2 = nb()
            for c in range(8):
                TR(bankbf(bt2)[0:64, c * 128:(c + 1) * 128], viT[:, c * 64:(c + 1) * 64], R=[Bt["viT"]], PW=[PB[bt2]])
            CP("dve", v_tm[0:64, tb * 1024:(tb + 1) * 1024], bankbf(bt2)[0:64, :], R=[PB[bt2]], PW=[Bt["v_tm"]])
        S.rotate(Bt["qtT"]); S.rotate(Bt["gsil"])
        for tb in range(2):
            bq, bg = cur
            if tb == 0:
                cur = P_qg(1)
            ACT(qs, bank(bq), AF.Silu, R=[PB[bq]], W=[Bt["qs"]])
            ACT(t1, bank(bg), AF.Silu, R=[PB[bg]], W=[Bt["t1"]])
            TS(gsil[:, tb * 512:(tb + 1) * 512], t1, pp[:, PP_OG + h:PP_OG + h + 1], None, ALU.mult, ALU.bypass,
               R=[Bt["t1"], Bpp], PW=[Bt["gsil"]])
            ACT(eqb, bown[:, tb * 512:(tb + 1) * 512], AF.Exp, R=[Bt["bown"]], W=[Bt["eqb"]])
            TT(qtT[:, tb * 512:(tb + 1) * 512], qs, eqb, ALU.mult, R=[Bt["qs"], Bt["eqb"]], PW=[Bt["qtT"]])
        w_prefetch()
        S.op("dve", lambda e: e.memset(Sst, 0.0), writes=[Bt["Sst"]])
        S.rotate(Bt["Sbf"])
        for c8 in range(4):
            bs = nb2()
            for cc in range(8):
                c = c8 * 8 + cc
                tgt = PD[bs // 2][:, cc * 128:(cc + 1) * 128]
                pbk = PB[bs + (cc // 4)]
                MM(tgt, kh_tm[0:64, c * 128:(c + 1) * 128], v_tm[0:64, c * 128:(c + 1) * 128], True, True,
                   R=[Bt["kh_tm"], Bt["v_tm"]], PW=[pbk])
            for cc in range(8):
                c = c8 * 8 + cc
                tgt = PD[bs // 2][:, cc * 128:(cc + 1) * 128]
                pbk = PB[bs + (cc // 4)]
                if c >= 16:
                    ACT(Sbf[:, (c - 16) * 128:(c - 15) * 128], Sst, AF.Copy, R=[Bt["Sst"], Bt["dl"]], PW=[Bt["Sbf"]],
                        scale=dl[:, 32 + c - 16: 33 + c - 16])
                STT(Sst, Sst, dl[:, c:c + 1], tgt, ALU.mult, ALU.add, R=[Bt["Sst"], Bt["dl"], pbk], W=[Bt["Sst"]])
        S.rotate(Bt["smT"])
        for half in range(2):
            bsc = nb()
            for cc in range(8):
                c = half * 8 + cc
                MM(bank(bsc)[0:64, cc * 64:(cc + 1) * 64], ktT[:, c * 64:(c + 1) * 64], qtT[:, c * 64:(c + 1) * 64],
                   True, True, R=[Bt["ktT"], Bt["qtT"]], PW=[PB[bsc]])
            TT(smT[0:64, half * 512:(half + 1) * 512], bank(bsc)[0:64, :], cmask[:], ALU.mult,
               R=[PB[bsc], Bconst], PW=[Bt["smT"]])
        for half in range(2):
            bo = nb()
            for cc in range(8):
                c = half * 8 + cc
                tgt = bank(bo)[:, cc * 64:(cc + 1) * 64]
                MM(tgt, v_tm[0:64, (16 + c) * 128:(17 + c) * 128], smT[0:64, c * 64:(c + 1) * 64], True, False,
                   R=[Bt["v_tm"], Bt["smT"]], PW=[PB[bo]])
                MM(tgt, Sbf[:, c * 128:(c + 1) * 128], qtT[:, c * 64:(c + 1) * 64], False, True,
                   R=[Bt["Sbf"], Bt["qtT"]], PW=[PB[bo]])
            ACT(osq, bank(bo), AF.Square, R=[PB[bo]], W=[Bt["osq"]])
            bn_ = nb()
            MM(bank(bn_), ones[:], osq, True, True, R=[Bconst, Bt["osq"]], PW=[PB[bn_]])
            ACT(rstd, bank(bn_), AF.Ln, R=[PB[bn_]], W=[Bt["rstd"]], scale=1.0 / 128, bias=epsc)
            ACT(rstd, rstd, AF.Exp, R=[Bt["rstd"]], W=[Bt["rstd"]], scale=-0.5)
            TT(t1, bank(bo), rstd, ALU.mult, R=[PB[bo], Bt["rstd"]], W=[Bt["t1"]])
            TT(yaT[:, h * 1024 + half * 512: h * 1024 + (half + 1) * 512], t1, gsil[:, half * 512:(half + 1) * 512],
               ALU.mult, R=[Bt["t1"], Bt["gsil"]], PW=[ByaT])
    if stop == "A":
        return finish([("yaT", yaT, [128, 8192], BF16, [ByaT]), ("hTo", hTo[:], [128, 16384], BF16, [BhTo]),
                       ("hTp", hTp, [128, 16384], BF16, [BhTp])])
    S.barrier()

    biasT = R4[:, 0:5120]
    BbiasT = Buf("biasT")
    S.dma("sp", biasT, biasd, slot="c3", writes=[BbiasT])
    for h in range(8):
        TT(biasT[:, h * 640:(h + 1) * 640], biasT[:, h * 640:(h + 1) * 640], smask, ALU.add,
           R=[BbiasT, Bcst], W=[BbiasT])
    qnT = r1f(0, 512).bitcast(BF16)
    knT = r1f(512, 768).bitcast(BF16)
    vT = r1f(1280, 768).bitcast(BF16)
    vtm = r1f(2048, 768).bitcast(BF16)
    sqb = r1f(2816, 256).bitcast(BF16)
    rsb = r1f(3072, 512)
    tmpS = r1f(3584, 640)
    Pb = r1f(4224, 320).bitcast(BF16)
    rinv = r1f(4544, 512)
    tmpS2 = r1f(5056, 640)
    Pb2 = r1f(5696, 320).bitcast(BF16)
    Bb = {n: Buf(n) for n in ["qnT", "knT", "vT", "vtm", "sqb", "rsb", "tmpS", "Pb", "rinv", "tmpS2", "Pb2"]}
    scale = 128 ** -0.5

    def headnorm(bz, n, gcol, dst, bdst):
        ACT(sqb[:, 0:n], bank(bz)[:, 0:n], AF.Square, R=[PB[bz]], W=[Bb["sqb"]])
        bn_ = nb()
        MM(bank(bn_)[:, 0:n], ones[:], sqb[:, 0:n], True, True, R=[Bconst, Bb["sqb"]], PW=[PB[bn_]])
        ACT(rsb[:, 0:n], bank(bn_)[:, 0:n], AF.Ln, R=[PB[bn_]], W=[Bb["rsb"]], scale=1.0 / 128, bias=epsc)
        ACT(rsb[:, 0:n], rsb[:, 0:n], AF.Exp, R=[Bb["rsb"]], W=[Bb["rsb"]], scale=-0.5)
        STT(dst, bank(bz)[:, 0:n], gcol, rsb[:, 0:n], ALU.mult, ALU.mult, R=[PB[bz], Bb["rsb"], Bpp], PW=[bdst])

    for h in range(8):
        wk, wkb = w_next()
        S.rotate(Bb["knT"]); S.rotate(Bb["vT"]); S.rotate(Bb["vtm"]); S.rotate(Bb["qnT"])
        def P_kv(tb_):
            a_ = nb()
            proj_fm(wk, wkb, 0, 256, 16, hsrc, 512 + tb_ * 512, 512, a_)
            b_ = nb()
            proj_fm(wk, wkb, 1, 256, 16, hsrc, 512 + tb_ * 512, 512, b_)
            return a_, b_

        def P_q(tb_):
            a_ = nb()
            proj_fm(wq, wqb, 0, 128, 16, hsrc, 1024 + tb_ * 512, 512, a_)
            return a_

        cur = P_kv(0)
        for tb in range(3):
            bk, bv = cur
            if tb < 2:
                cur = P_kv(tb + 1)
            else:
                w_prefetch()
                wq, wqb = w_next()
                cur = P_q(0)
            headnorm(bk, 512, pp[:, PP_KG:PP_KG + 1], knT[:, tb * 512:(tb + 1) * 512], Bb["knT"])
            CP("act", vT[:, tb * 512:(tb + 1) * 512], bank(bv), R=[PB[bv]], PW=[Bb["vT"]])
        for tb in range(2):
            bq = cur
            if tb == 0:
                cur = P_q(1)
            headnorm(bq, 512, pp[:, PP_QG:PP_QG + 1], qnT[:, tb * 512:(tb + 1) * 512], Bb["qnT"])
        w_prefetch()
        for half in range(2):
            bt_ = nb()
            nblk = 8 if half == 0 else 4
            for j in range(nblk):
                kb = half * 8 + j
                TR(bankbf(bt_)[:, j * 128:(j + 1) * 128], vT[:, kb * 128:(kb + 1) * 128], R=[Bb["vT"]], PW=[PB[bt_]])
            CP(alt(), vtm[:, half * 1024: half * 1024 + nblk * 128], bankbf(bt_)[:, 0:nblk * 128],
               R=[PB[bt_]], PW=[Bb["vtm"]])
        sc_bank_of = [2, 4, 6, 2, 6, 0, 2, 6]

        def scores(n):
            b2 = sc_bank_of[n]
            S.rotate(PB[b2]); S.rotate(PB[b2 + 1])
            stile_ = PD[b2 // 2]
            for r in range(5):
                kb = n + 4 - r
                MM(stile_[:, r * 128:(r + 1) * 128], knT[:, kb * 128:(kb + 1) * 128], qnT[:, n * 128:(n + 1) * 128],
                   True, True, R=[Bb["knT"], Bb["qnT"]], PW=[PB[b2 + (r // 4)]])

        scores(0)
        for i in range(8):
            half, qq = i // 4, i % 4
            bo, br = (0, 1) if half == 0 else (4, 5)
            if qq == 0:
                S.rotate(PB[bo]); S.rotate(PB[br])
            if i + 1 < 8:
                scores(i + 1)
            b2 = sc_bank_of[i]
            stile = PD[b2 // 2]
            tS, BtS = (tmpS, Bb["tmpS"]) if i % 2 == 0 else (tmpS2, Bb["tmpS2"])
            Pq, BPq = (Pb, Bb["Pb"]) if i % 2 == 0 else (Pb2, Bb["Pb2"])
            STT(tS, stile[:, 0:640], scale, biasT[:, h * 640:(h + 1) * 640], ALU.mult, ALU.add,
                R=[PB[b2], PB[b2 + 1], BbiasT], W=[BtS])
            npre = max(0, 4 - i)
            ACT(Pq[:, 0:(5 - npre) * 128], tS[:, 0:(5 - npre) * 128], AF.Exp, R=[BtS], W=[BPq])
            if npre > 0:
                ACT(Pq[:, (5 - npre) * 128:640], tS[:, (5 - npre) * 128:640], AF.Exp,
                    R=[BtS, Bpp], PW=[BPq], bias=pp[:, PP_PM:PP_PM + 1])
            for r in range(5):
                kb = i + 4 - r
                MM(bank(bo)[:, qq * 128:(qq + 1) * 128], vtm[:, kb * 128:(kb + 1) * 128], Pq[:, r * 128:(r + 1) * 128],
                   r == 0, r == 4, R=[Bb["vtm"], BPq], PW=[PB[bo]])
            for r in range(5):
                MM(bank(br)[:, qq * 128:(qq + 1) * 128], ones[:], Pq[:, r * 128:(r + 1) * 128],
                   r == 0, r == 4, R=[Bconst, BPq], PW=[PB[br]])
            if qq == 3:
                S.op("dve", lambda e, br=br: e.reciprocal(out=rinv, in_=bank(br)), reads=[PB[br]], writes=[Bb["rinv"]])
                TT(ybT[:, h * 1024 + half * 512: h * 1024 + (half + 1) * 512], bank(bo), rinv, ALU.mult,
                   R=[PB[bo], Bb["rinv"]], PW=[BybT])
    if stop == "B":
        return finish([("yaT", yaT, [128, 8192], BF16, [ByaT]), ("ybT", ybT, [128, 8192], BF16, [BybT])])
    S.barrier()

    mT = R4.bitcast(BF16)
    BmT = Buf("mT")
    hb = r1f(0, 32)
    tha = r1f(64, 512)
    thb = r1f(576, 512)
    ma = r1f(1088, 512)
    mb = r1f(1600, 512)
    Bc = {n: Buf(n) for n in ["hb", "tha", "thb", "ma", "mb"]}
    TS(hb, pp[:, PP_BGA:PP_BGA + 32], 0.5, None, ALU.mult, ALU.bypass, R=[Bpp], W=[Bc["hb"]])

    def ysrc(base, bb):
        def f(kc, t0, n):
            return base[:, kc * 1024 + t0: kc * 1024 + t0 + n], bb
        return f

    def hown(kc, t0, n):
        return hTo[:, kc * 1024 + t0: kc * 1024 + t0 + n], BhTo

    for c in range(16):
        wg, wgb = w_next()
        res = {}
        for half in range(2):
            b_ga = nb()
            proj_fm(wg, wgb, 0, 256, 16, hown, half * 512, 512, b_ga)
            b_gb = nb()
            proj_fm(wg, wgb, 1, 256, 16, hown, half * 512, 512, b_gb)
            res[half] = (b_ga, b_gb)
        w_prefetch()
        wp, wpb = w_next()
        for half in range(2):
            b_ga, b_gb = res[half]
            b_pa = nb()
            proj_fm(wp, wpb, 0, 256, 8, ysrc(yaT, ByaT), half * 512, 512, b_pa)
            b_pb = nb()
            proj_fm(wp, wpb, 1, 256, 8, ysrc(ybT, BybT), half * 512, 512, b_pb)
            ACT(tha, bank(b_ga), AF.Tanh, R=[PB[b_ga], Bc["hb"]], W=[Bc["tha"]], scale=0.5, bias=hb[:, c:c + 1])
            ACT(thb, bank(b_gb), AF.Tanh, R=[PB[b_gb], Bc["hb"]], W=[Bc["thb"]], scale=0.5, bias=hb[:, 16 + c:17 + c])
            STT(ma, tha, 1.0, bank(b_pa), ALU.add, ALU.mult, R=[Bc["tha"], PB[b_pa]], W=[Bc["ma"]])
            STT(mb, thb, 1.0, bank(b_pb), ALU.add, ALU.mult, R=[Bc["thb"], PB[b_pb]], W=[Bc["mb"]])
            TT(mT[:, c * 1024 + half * 512: c * 1024 + (half + 1) * 512], ma, mb, ALU.add,
               R=[Bc["ma"], Bc["mb"]], PW=[BmT])
        w_prefetch()
    if stop == "C":
        return finish([("mT", mT, [128, 16384], BF16, [BmT])])
    S.barrier()

    X = R1
    for r in range(8):
        S.dma("sp", X[:, r * 2048:(r + 1) * 2048], xo[r * 128:(r + 1) * 128, :], slot=f"xd{r}", writes=[BX[r]])
    for g in range(8):
        wd, wdb = w_next()
        for r in range(8):
            b = nb()
            for kc in range(16):
                MM(bank(b)[:, 0:256], mT[:, kc * 1024 + r * 128: kc * 1024 + (r + 1) * 128],
                   wd[:, kc * 256:(kc + 1) * 256], kc == 0, kc == 15, R=[BmT, wdb], PW=[PB[b]])
            xs = X[:, r * 2048 + g * 256: r * 2048 + (g + 1) * 256]
            STT(xs, bank(b)[:, 0:256], 0.5, xs, ALU.mult, ALU.add, R=[PB[b], BX[r]], W=[BX[r]])
        w_prefetch()
    if stop == "D":
        return finish([("X1", X[:], [128, 16384], F32, BX)])
    S.barrier()

    S.rotate(BhTo)
    norm_phase([("X", r) for r in range(8)], 1, None)
    S.barrier()

    actb = R3
    Bact = Buf("act")
    sil = R4[:, 0:512]
    Bsil = Buf("sil")
    last = []
    for g in range(NFC // FG):
        S.rotate(Bact)
        for j in range(FG):
            wf, wfb = w_next()
            for half in range(2):
                bg = nb()
                proj_fm(wf, wfb, 0, 256, 16, hown, half * 512, 512, bg)
                bu = nb()
                proj_fm(wf, wfb, 1, 256, 16, hown, half * 512, 512, bu)
                ACT(sil, bank(bg), AF.Silu, R=[PB[bg]], W=[Bsil])
                TT(actb[:, j * 1024 + half * 512: j * 1024 + (half + 1) * 512], sil, bank(bu), ALU.mult,
                   R=[Bsil, PB[bu]], PW=[Bact])
            w_prefetch()
        for cg in range(8):
            w2, w2b = w_next()
            for r in range(8):
                b = nb()
                for j in range(FG):
                    MM(bank(b)[:, 0:256], actb[:, j * 1024 + r * 128: j * 1024 + (r + 1) * 128],
                       w2[:, j * 256:(j + 1) * 256], j == 0, j == FG - 1, R=[Bact, w2b], PW=[PB[b]])
                xs = X[:, r * 2048 + cg * 256: r * 2048 + (cg + 1) * 256]
                TT(xs, bank(b)[:, 0:256], xs, ALU.add, R=[PB[b], BX[r]], W=[BX[r]])
            w_prefetch()
    for r in range(8):
        last.append(S.dma("sp", yout[r * 128:(r + 1) * 128, :], X[:, r * 2048:(r + 1) * 2048], slot=f"o{r}",
                          reads=[BX[r]]))
    S.emit(final_waits=last)
    return nc, []


D_A = 1024
D_B = 1024


def _wtile(wcols, kcn):
    K, C = wcols.shape
    assert K == kcn * 128
    return np.ascontiguousarray(wcols.reshape(kcn, 128, C).transpose(1, 0, 2).reshape(128, kcn * C))


def _prep_shared(inp):
    w_in = np.asarray(inp["w_in"], np.float32)[0]
    c_q, c_f, c_i, c_g = 0, D_A, 2 * D_A, 3 * D_A
    c_qb, c_kb, c_vb = 4 * D_A, 4 * D_A + D_B, 4 * D_A + 2 * D_B
    c_ga, c_gb = 4 * D_A + 3 * D_B, 4 * D_A + 3 * D_B + D
    sl = lambda c0, h: w_in[:, c0 + h * 128: c0 + (h + 1) * 128]
    wA = np.stack([_wtile(np.concatenate(p, axis=1), 16) for h in range(8)
                   for p in ((sl(c_f, h), sl(c_i, h)), (sl(c_q, h), sl(c_g, h)))])
    wBkv = np.stack([_wtile(np.concatenate((sl(c_kb, h), sl(c_vb, h)), axis=1), 16) for h in range(8)])
    wBq = np.stack([_wtile(sl(c_qb, h), 16) for h in range(8)])
    wCg = np.stack([_wtile(np.concatenate((sl(c_ga, c), sl(c_gb, c)), axis=1), 16) for c in range(16)])
    wpa = np.asarray(inp["w_proj_a"], np.float32)[0]
    wpb = np.asarray(inp["w_proj_b"], np.float32)[0]
    wCp = np.stack([_wtile(np.concatenate((wpa[:, c * 128:(c + 1) * 128], wpb[:, c * 128:(c + 1) * 128]), axis=1), 8)
                    for c in range(16)])
    wo = np.asarray(inp["w_out"], np.float32)[0]
    wD = np.stack([_wtile(wo[:, g * 256:(g + 1) * 256], 16) for g in range(8)])
    wfi = np.asarray(inp["w_ffn_in"], np.float32)[0]
    wF1 = np.stack([_wtile(np.concatenate((wfi[:, j * 128:(j + 1) * 128], wfi[:, DFF + j * 128: DFF + (j + 1) * 128]),
                                          axis=1), 16) for j in range(NFC)])
    wfo = np.asarray(inp["w_ffn_out"], np.float32)[0]
    wF2 = np.stack([np.stack([_wtile(wfo[g * FG * 128:(g + 1) * FG * 128, cg * 256:(cg + 1) * 256], FG)
                              for cg in range(8)]) for g in range(NFC // FG)])
    pp = np.zeros((128, NPP), np.float32)
    lbl = np.asarray(inp["hgrn_lb_logits"], np.float32)
    pp[:, PP_L0:PP_L0 + 8] = lbl[0].reshape(8, 128).T
    pp[:, PP_L1:PP_L1 + 8] = lbl[1].reshape(8, 128).T
    pp[:, PP_OG:PP_OG + 8] = np.asarray(inp["hgrn_out_gain"], np.float32)[0].reshape(8, 128).T
    bg = np.asarray(inp["b_gate"], np.float32)[0]
    pp[:, PP_BGA:PP_BGA + 16] = bg[:D].reshape(16, 128).T
    pp[:, PP_BGB:PP_BGB + 16] = bg[D:].reshape(16, 128).T
    pp[:, PP_QG] = np.asarray(inp["q_gain"], np.float32)[0]
    pp[:, PP_KG] = np.asarray(inp["k_gain"], np.float32)[0]
    pp[:, PP_EPS] = 1e-6
    gbc = np.stack([np.broadcast_to(np.asarray(inp["norm_mix"], np.float32)[0][None, :], (128, D)),
                    np.broadcast_to(np.asarray(inp["norm_ffn"], np.float32)[0][None, :], (128, D))]).copy()
    cst = np.zeros((128, 128 + 128 + 512 + 512 + 640), np.float32)
    cst[:, 0:128] = np.eye(128, dtype=np.float32)
    cst[:, 128:256] = 1.0
    cm = np.triu(np.ones((64, 64), np.float32))
    cst[0:64, 256:768] = np.tile(cm, (1, 8))
    sm = np.ones((128, 512), np.float32)
    sm[:, 0::64] = 0.0
    cst[:, 768:1280] = sm
    kl = np.arange(128)[:, None, None]
    r = np.arange(5)[None, :, None]
    ql = np.arange(128)[None, None, :]
    dist = (ql + 128 * r) - kl
    kchunk_rel = (2 * 0 - 2 * r) + (kl // 64)
    qchunk_rel = ql // 64
    dchunk = qchunk_rel - kchunk_rel
    valid = (dchunk >= 0) & (dchunk <= 8)
    smask = np.where(valid, 0.0, NEG).astype(np.float32)
    cst[:, 1280:1920] = smask.reshape(128, 640)
    ridx = np.clip(dist, -63, 127) + 63
    rb = np.asarray(inp["rel_bias"], np.float32)[0]
    biasT = np.concatenate([rb[hh][ridx].reshape(128, 640) for hh in range(8)], axis=1)
    return dict(wA=wA, wBkv=wBkv, wBq=wBq, wCg=wCg, wCp=wCp, wD=wD, wF1=wF1, wF2=wF2, gbc=gbc, cst=cst,
                biasT=np.ascontiguousarray(biasT, np.float32)), pp


_CACHE = {}


def kernel(**inputs):
    x = np.asarray(inputs["x"], np.float32)
    shared, pp = _prep_shared(inputs)
    in_maps = []
    for c in range(8):
        b, half = c // 2, c % 2
        m = dict(shared)
        m["xo"] = np.ascontiguousarray(x[b, half * T:(half + 1) * T])
        m["xp"] = np.ascontiguousarray(x[b, 0:T]) if half == 1 else np.zeros((T, D), np.float32)
        ppc = pp.copy()
        ppc[:, PP_PM] = 0.0 if half == 1 else NEG
        m["pp"] = ppc
        in_maps.append(m)
    if "nc" not in _CACHE:
        _CACHE["nc"] = build(None)[0]
    res = run_bass_kernel_spmd(_CACHE["nc"], in_maps, core_ids=list(range(8)))
    out = np.zeros((4, 2048, D), np.float32)
    for c in range(8):
        b, half = c // 2, c % 2
        out[b, half * T:(half + 1) * T] = res.results[c]["y"]
    return out
```
